# Optimizing a Trainium2 kernel written in Bass

```python
import math
import jax
import jax.numpy as jnp
from jax import lax
import numpy as np

D_MODEL = 1024
BATCH = 8
SEQ = 4096
DEPTH = 2

GRID_W = 64
CTX_LEN = 256
DA_HEADS = 6
DA_DK = 64
DA_DV = 2 * DA_DK
FN_GROUPS = 4
FN_CH = 64
A_COLS = DA_HEADS * (4 * DA_DK + DA_DV)
B_COLS = FN_GROUPS * FN_CH
ML_INNER = 2 * D_MODEL
ML_HEADS = 8
ML_DH = ML_INNER // ML_HEADS
ML_BLOCK = 4
ML_CONV_W = 5
ML_CHUNK = 128
D_FF = -(-(8 * D_MODEL) // (3 * 256)) * 256
ALPHA = (2 * DEPTH) ** 0.25
BETA = (8 * DEPTH) ** -0.25
ROPE_BASE = 10000.0
Q_BLOCK = 128
LN_EPS = 1e-5
N_EVEN = (DEPTH + 1) // 2
N_ODD = DEPTH // 2

kernel_name = 'hybrid_diffattn_fnet_mlstm_block'


def _ln(x):
    xf = x.astype(jnp.float32)
    mu = jnp.mean(xf, -1, keepdims=True)
    var = jnp.mean(jnp.square(xf - mu), -1, keepdims=True)
    return (xf - mu) * lax.rsqrt(var + LN_EPS)


def post_norm(x, y, g, b):
    return (_ln(ALPHA * x + y) * g + b).astype(x.dtype)


def modulate(x, shift, scale):
    return (_ln(x) * (1.0 + scale) + shift).astype(x.dtype)


def swiglu(u, w1, w3, w2):
    return (jax.nn.silu(u @ w1) * (u @ w3)) @ w2


def axial_rope(n_tok, head_dim):
    rows = n_tok // GRID_W
    row = jnp.repeat(jnp.arange(rows, dtype=jnp.float32), GRID_W)
    col = jnp.tile(jnp.arange(GRID_W, dtype=jnp.float32), rows)
    n_freq = head_dim // 4
    inv_freq = ROPE_BASE ** (-jnp.arange(n_freq, dtype=jnp.float32) / n_freq)
    ang = jnp.concatenate([row[:, None] * inv_freq, col[:, None] * inv_freq], -1)
    return jnp.cos(ang), jnp.sin(ang)


def apply_rope(x, cos, sin):
    half = x.shape[-1] // 2
    x1, x2 = x[..., :half], x[..., half:]
    cs = cos[None, :, None, None, :].astype(x.dtype)
    sn = sin[None, :, None, None, :].astype(x.dtype)
    return jnp.concatenate([x1 * cs - x2 * sn, x1 * sn + x2 * cs], -1)


def split_even(u, w_in):
    p = u @ w_in
    b, n = p.shape[:2]
    pa = p[..., :A_COLS].reshape(b, n, DA_HEADS, 4 * DA_DK + DA_DV)
    q = pa[..., :2 * DA_DK].reshape(b, n, DA_HEADS, 2, DA_DK)
    k = pa[..., 2 * DA_DK:4 * DA_DK].reshape(b, n, DA_HEADS, 2, DA_DK)
    v = pa[..., 4 * DA_DK:]
    f = p[..., A_COLS:].reshape(b, n, FN_GROUPS, FN_CH)
    return q, k, v, f


def diff_softmax_mix(q, keys, vals, lam):
    s = jnp.einsum('bqhmd,bkhmd->bhmqk', q, keys).astype(jnp.float32) * (DA_DK ** -0.5)
    p = jax.nn.softmax(s, axis=-1)
    a = p[:, :, 0] - lam * p[:, :, 1]
    return jnp.einsum('bhqk,bkhd->bqhd', a.astype(vals.dtype), vals)


def diff_attn_blocks(q, keys, vals, lam):
    b, n = q.shape[:2]
    nb = n // Q_BLOCK
    qb = jnp.moveaxis(q.reshape(b, nb, Q_BLOCK, DA_HEADS, 2, DA_DK), 1, 0)
    out = lax.map(lambda qq: diff_softmax_mix(qq, keys, vals, lam), qb)
    return jnp.moveaxis(out, 0, 1).reshape(b, n, DA_HEADS, DA_DV)


def fourier_mix(f):
    ff = jnp.fft.fft2(jnp.swapaxes(f, 1, 2).astype(jnp.float32), norm='ortho').real
    return jnp.swapaxes(ff, 1, 2).astype(f.dtype)


def merge_even(a, f, head_g, lam_init):
    b, n = a.shape[:2]
    af = a.astype(jnp.float32)
    a_n = af * lax.rsqrt(jnp.mean(jnp.square(af), -1, keepdims=True) + LN_EPS) * head_g * (1.0 - lam_init)
    return jnp.concatenate([a_n.astype(f.dtype).reshape(b, n, DA_HEADS * DA_DV), f.reshape(b, n, B_COLS)], -1)


def even_mixer(u, uc, cos, sin, w_in, w_out, lq1, lk1, lq2, lk2, head_g, lam_init, need_ctx):
    q, k, v, f = split_even(u, w_in)
    qc, kc, vc, fc = split_even(uc, w_in)
    lam = (jnp.exp(jnp.sum(lq1 * lk1).astype(jnp.float32))
           - jnp.exp(jnp.sum(lq2 * lk2).astype(jnp.float32)) + lam_init)
    keys = jnp.concatenate([apply_rope(k, cos, sin), kc], axis=1)
    vals = jnp.concatenate([v, vc], axis=1)
    a = diff_attn_blocks(apply_rope(q, cos, sin), keys, vals, lam)
    y = merge_even(a, fourier_mix(f), head_g, lam_init) @ w_out
    yc = None
    if need_ctx:
        ac = diff_softmax_mix(qc, kc, vc, lam)
        yc = merge_even(ac, fourier_mix(fc), head_g, lam_init) @ w_out
    return y, yc


def centred_conv(x, w, b):
    y = lax.conv_general_dilated(x, w[:, None, :], window_strides=(1,),
                                 padding=[(ML_CONV_W // 2, ML_CONV_W // 2)],
                                 dimension_numbers=('NWC', 'WIO', 'NWC'),
                                 feature_group_count=x.shape[-1])
    return y + b


def block_diag(x, w):
    b, n, _ = x.shape
    return jnp.einsum('bngi,gij->bngj', x.reshape(b, n, -1, ML_BLOCK), w).reshape(b, n, ML_INNER)


def to_heads(x):
    b, n, _ = x.shape
    return x.reshape(b, n, ML_HEADS, ML_DH).transpose(0, 2, 1, 3)


def mlstm_features(u, w_in, conv_w, conv_b, wq, wk, wv, w_ig, b_ig, w_fg, b_fg):
    p = u @ w_in
    xm, z = p[..., :ML_INNER], p[..., ML_INNER:]
    xc = jax.nn.silu(centred_conv(xm, conv_w, conv_b))
    q, k, v = block_diag(xc, wq), block_diag(xc, wk), block_diag(xm, wv)
    g_in = jnp.concatenate([q, k, v], -1)
    log_i = (jnp.einsum('bnc,dch->dbhn', g_in, w_ig) + b_ig[:, None, :, None]).astype(jnp.float32)
    log_f = jax.nn.log_sigmoid(
        (jnp.einsum('bnc,dch->dbhn', g_in, w_fg) + b_fg[:, None, :, None]).astype(jnp.float32))
    return to_heads(q), to_heads(k) * (ML_DH ** -0.5), to_heads(v), log_i, log_f, xc, z


def zero_state(b):
    return (jnp.zeros((b, ML_HEADS, ML_DH, ML_DH), jnp.float32),
            jnp.zeros((b, ML_HEADS, ML_DH), jnp.float32),
            jnp.zeros((b, ML_HEADS), jnp.float32))


def mlstm_scan(q, k, v, log_i, log_f, state, with_output):
    b, h, n, dh = q.shape
    nc = n // ML_CHUNK

    def chunks(a):
        return jnp.moveaxis(a.reshape(b, h, nc, ML_CHUNK, *a.shape[3:]), 2, 0).astype(jnp.float32)

    xs = (chunks(q), chunks(k), chunks(v), chunks(log_i), chunks(log_f))
    ordered = jnp.tril(jnp.ones((ML_CHUNK, ML_CHUNK), dtype=bool))

    def step(carry, inp):
        C, nv, m = carry
        qc, kc, vc, li, lf = inp
        bcum = jnp.cumsum(lf, axis=-1)
        b_end = bcum[..., -1]
        hc = None
        if with_output:
            dmat = jnp.where(ordered, bcum[..., :, None] - bcum[..., None, :] + li[..., None, :], -jnp.inf)
            inter = bcum + m[..., None]
            m_t = jnp.maximum(jnp.max(dmat, -1), inter)
            dw = jnp.exp(dmat - m_t[..., None])
            iw = jnp.exp(inter - m_t)
            s = jnp.einsum('bhld,bhsd->bhls', qc, kc) * dw
            num = iw[..., None] * jnp.einsum('bhld,bhde->bhle', qc, C) + jnp.einsum('bhls,bhse->bhle', s, vc)
            den = iw * jnp.einsum('bhld,bhd->bhl', qc, nv) + jnp.sum(s, -1)
            hc = num / jnp.maximum(jnp.abs(den), jnp.exp(-m_t))[..., None]
        w_end = b_end[..., None] - bcum + li
        m_new = jnp.maximum(b_end + m, jnp.max(w_end, -1))
        keep = jnp.exp(b_end + m - m_new)
        w = jnp.exp(w_end - m_new[..., None])
        C_new = keep[..., None, None] * C + jnp.einsum('bhl,bhld,bhle->bhde', w, kc, vc)
        n_new = keep[..., None] * nv + jnp.einsum('bhl,bhld->bhd', w, kc)
        return (C_new, n_new, m_new), hc

    state, hs = lax.scan(step, state, xs)
    if with_output:
        hs = jnp.moveaxis(hs, 0, 2).reshape(b, h, n, dh)
    return hs, state


def flip_t(a, rev):
    return jnp.flip(a, axis=2) if rev else a


def mlstm_out(h, xc, z, skip, head_g, w_out):
    b, _, n, _ = h.shape
    hn = _ln(h).transpose(0, 2, 1, 3).reshape(b, n, ML_INNER) * head_g
    y = (hn + skip * xc) * jax.nn.silu(z)
    return y.astype(xc.dtype) @ w_out


def odd_mixer(u, uc, w_in, w_out, conv_w, conv_b, wq, wk, wv, w_ig, b_ig, w_fg, b_fg, skip, head_g, need_ctx):
    q, k, v, li, lf, xc, z = mlstm_features(u, w_in, conv_w, conv_b, wq, wk, wv, w_ig, b_ig, w_fg, b_fg)
    qc, kc, vc, lic, lfc, xcc, zc = mlstm_features(uc, w_in, conv_w, conv_b, wq, wk, wv, w_ig, b_ig, w_fg, b_fg)
    h_lat = 0.0
    h_ctx = 0.0
    for d in range(2):
        rev = d == 1
        hcd, st = mlstm_scan(flip_t(qc, rev), flip_t(kc, rev), flip_t(vc, rev),
                             flip_t(lic[d], rev), flip_t(lfc[d], rev), zero_state(uc.shape[0]), need_ctx)
        hld, _ = mlstm_scan(flip_t(q, rev), flip_t(k, rev), flip_t(v, rev),
                            flip_t(li[d], rev), flip_t(lf[d], rev), st, True)
        h_lat = h_lat + flip_t(hld, rev)
        if need_ctx:
            h_ctx = h_ctx + flip_t(hcd, rev)
    y = mlstm_out(h_lat, xc, z, skip, head_g, w_out)
    yc = mlstm_out(h_ctx, xcc, zc, skip, head_g, w_out) if need_ctx else None
    return y, yc


def setup_inputs(seed: int = 0) -> dict:
    key = jax.random.key(seed)
    ks = jax.random.split(key, 32)
    d = D_MODEL

    def nrm(k, shape, s):
        return s * jax.random.normal(k, shape, jnp.float32)

    return {
        'x': nrm(ks[0], (BATCH, SEQ, d), 1.0),
        'c': nrm(ks[1], (BATCH, d), 1.0),
        'ctx': nrm(ks[2], (BATCH, CTX_LEN, d), 1.0),
        'c_ctx': nrm(ks[3], (d,), 1.0),
        'w_mod': nrm(ks[4], (DEPTH, d, 6 * d), 0.5 * d ** -0.5),
        'b_mod': nrm(ks[5], (DEPTH, 6 * d), 0.01),
        'ln_g': 1.0 + nrm(ks[6], (DEPTH, 2, d), 0.01),
        'ln_b': nrm(ks[7], (DEPTH, 2, d), 0.01),
        'w_ff1': nrm(ks[8], (DEPTH, d, D_FF), d ** -0.5),
        'w_ff3': nrm(ks[9], (DEPTH, d, D_FF), d ** -0.5),
        'w_ff2': nrm(ks[10], (DEPTH, D_FF, d), BETA * D_FF ** -0.5),
        'a_w_in': nrm(ks[11], (N_EVEN, d, A_COLS + B_COLS), d ** -0.5),
        'a_w_out': nrm(ks[12], (N_EVEN, DA_HEADS * DA_DV + B_COLS, d), BETA * (DA_HEADS * DA_DV + B_COLS) ** -0.5),
        'da_lq1': nrm(ks[13], (N_EVEN, DA_DK), 0.1),
        'da_lk1': nrm(ks[14], (N_EVEN, DA_DK), 0.1),
        'da_lq2': nrm(ks[15], (N_EVEN, DA_DK), 0.1),
        'da_lk2': nrm(ks[16], (N_EVEN, DA_DK), 0.1),
        'da_head_g': 1.0 + nrm(ks[17], (N_EVEN, DA_DV), 0.01),
        'm_w_in': nrm(ks[18], (N_ODD, d, 2 * ML_INNER), d ** -0.5),
        'm_w_out': nrm(ks[19], (N_ODD, ML_INNER, d), BETA * ML_INNER ** -0.5),
        'm_conv_w': nrm(ks[20], (N_ODD, ML_CONV_W, ML_INNER), ML_CONV_W ** -0.5),
        'm_conv_b': nrm(ks[21], (N_ODD, ML_INNER), 0.01),
        'm_wq': nrm(ks[22], (N_ODD, ML_INNER // ML_BLOCK, ML_BLOCK, ML_BLOCK), ML_BLOCK ** -0.5),
        'm_wk': nrm(ks[23], (N_ODD, ML_INNER // ML_BLOCK, ML_BLOCK, ML_BLOCK), ML_BLOCK ** -0.5),
        'm_wv': nrm(ks[24], (N_ODD, ML_INNER // ML_BLOCK, ML_BLOCK, ML_BLOCK), ML_BLOCK ** -0.5),
        'm_w_ig': nrm(ks[25], (N_ODD, 2, 3 * ML_INNER, ML_HEADS), 0.1 * (3 * ML_INNER) ** -0.5),
        'm_b_ig': nrm(ks[26], (N_ODD, 2, ML_HEADS), 0.1),
        'm_w_fg': nrm(ks[27], (N_ODD, 2, 3 * ML_INNER, ML_HEADS), 0.1 * (3 * ML_INNER) ** -0.5),
        'm_b_fg': jnp.linspace(3.0, 6.0, ML_HEADS)[None, None, :] + nrm(ks[28], (N_ODD, 2, ML_HEADS), 0.01),
        'm_skip': 1.0 + nrm(ks[29], (N_ODD, ML_INNER), 0.01),
        'm_head_g': 1.0 + nrm(ks[30], (N_ODD, ML_INNER), 0.01),
    }


def reference(x, c, ctx, c_ctx, w_mod, b_mod, ln_g, ln_b, w_ff1, w_ff3, w_ff2,
              a_w_in, a_w_out, da_lq1, da_lk1, da_lq2, da_lk2, da_head_g,
              m_w_in, m_w_out, m_conv_w, m_conv_b, m_wq, m_wk, m_wv,
              m_w_ig, m_b_ig, m_w_fg, m_b_fg, m_skip, m_head_g):
    cos, sin = axial_rope(x.shape[1], DA_DK)
    h, hc = x, ctx
    for l in range(DEPTH):
        need_ctx = l < DEPTH - 1
        i = l // 2
        mod = jnp.split((jax.nn.silu(c) @ w_mod[l] + b_mod[l])[:, None, :], 6, axis=-1)
        mod_c = jnp.split(jax.nn.silu(c_ctx) @ w_mod[l] + b_mod[l], 6, axis=-1)
        u = modulate(h, mod[0], mod[1])
        uc = modulate(hc, mod_c[0], mod_c[1])
        if l % 2 == 0:
            lam_init = 0.8 - 0.6 * math.exp(-0.3 * l)
            y, yc = even_mixer(u, uc, cos, sin, a_w_in[i], a_w_out[i], da_lq1[i], da_lk1[i],
                               da_lq2[i], da_lk2[i], da_head_g[i], lam_init, need_ctx)
        else:
            y, yc = odd_mixer(u, uc, m_w_in[i], m_w_out[i], m_conv_w[i], m_conv_b[i], m_wq[i], m_wk[i], m_wv[i],
                              m_w_ig[i], m_b_ig[i], m_w_fg[i], m_b_fg[i], m_skip[i], m_head_g[i], need_ctx)
        h = post_norm(h, mod[2] * y, ln_g[l, 0], ln_b[l, 0])
        h = post_norm(h, mod[5] * swiglu(modulate(h, mod[3], mod[4]), w_ff1[l], w_ff3[l], w_ff2[l]),
                      ln_g[l, 1], ln_b[l, 1])
        if need_ctx:
            hc = post_norm(hc, mod_c[2] * yc, ln_g[l, 0], ln_b[l, 0])
            hc = post_norm(hc, mod_c[5] * swiglu(modulate(hc, mod_c[3], mod_c[4]), w_ff1[l], w_ff3[l], w_ff2[l]),
                           ln_g[l, 1], ln_b[l, 1])
    return h
```

```python
import math
import numpy as np
import ml_dtypes
import concourse.bass as bass
import concourse.mybir as mybir
from concourse.bass_utils import run_bass_kernel_spmd

F32 = mybir.dt.float32
BF16 = mybir.dt.bfloat16
U8 = mybir.dt.uint8
ALU = mybir.AluOpType
AF = mybir.ActivationFunctionType

D = 1024
NL = 4096
NCX = 256
NT = NL + NCX
NTILE = NT // 128
DFF = 2816
ALPHA = 4 ** 0.25
LN_EPS = 1e-5
LAM_INIT = 0.8 - 0.6 * math.exp(0.0)
M3_STEPS = 9


class FW:
    NS = 8

    def __init__(self, nc):
        self.nc = nc
        self.engs = ('pe', 'dve', 'act', 'pool', 'sp')
        self.csem = {e: nc.alloc_semaphore("c_" + e) for e in ('pe', 'dve', 'act', 'pool')}
        self.ccnt = {e: 0 for e in self.csem}
        self.pending = {e: False for e in self.csem}
        self.dq = {q: dict(sems=[nc.alloc_semaphore(f"d_{q}{i}") for i in range(self.NS)], n=0)
                   for q in ('sp', 'pool', 'act')}
        self.known = {e: {} for e in self.engs}
        self.buf = {}
        self.prog = {e: [] for e in self.engs}
        self.ninstr = 0

    def _deps(self, reads, writes):
        ev = {}

        def add(e):
            if e is None:
                return
            s, v = e
            if s.num not in ev or ev[s.num][1] < v:
                ev[s.num] = (s, v)
        for r in reads:
            b = self.buf.get(r)
            if b:
                add(b['w'])
        for w in writes:
            b = self.buf.get(w)
            if b:
                add(b['w'])
                for e in b['r'].values():
                    add(e)
        return ev

    def _wait(self, e, ev, skip_sem=None):
        kn = self.known[e]
        for k, (s, v) in ev.items():
            if skip_sem is not None and s is skip_sem:
                continue
            if kn.get(k, 0) < v:
                self.prog[e].append(('w', s, v))
                kn[k] = v

    def _record(self, reads, writes, event):
        for r in reads:
            b = self.buf.setdefault(r, dict(w=None, r={}))
            b['r'][event[0].num] = event
        for w in writes:
            self.buf[w] = dict(w=event, r={})

    def op(self, e, fn, reads=(), writes=(), signal=True):
        ev = self._deps(reads, writes)
        self._wait(e, ev, skip_sem=self.csem['pe'] if e == 'pe' else None)
        self.ninstr += 1
        if signal:
            self.ccnt[e] += 1
            self.prog[e].append(('i', fn, self.csem[e]))
            event = (self.csem[e], self.ccnt[e])
            self.pending[e] = False
        else:
            self.prog[e].append(('i', fn, None))
            event = (self.csem[e], self.ccnt[e] + 1)
            self.pending[e] = True
        self._record(reads, writes, event)

    def dma(self, q, out, in_, reads=(), writes=(), **kw):
        ev = self._deps(reads, writes)
        self._wait(q, ev)
        d = self.dq[q]
        i = d['n'] % self.NS
        rnd = d['n'] // self.NS
        sem = d['sems'][i]
        if rnd > 0:
            self._wait(q, {sem.num: (sem, 16 * rnd)})
        self.prog[q].append(('d', out, in_, kw, sem))
        self.ninstr += 1
        d['n'] += 1
        self._record(reads, writes, (sem, 16 * (rnd + 1)))

    def barrier(self):
        evs = {}
        for e, s in self.csem.items():
            assert not self.pending[e], e
            if self.ccnt[e] > 0:
                evs[s.num] = (s, self.ccnt[e])
        for q, d in self.dq.items():
            for i, s in enumerate(d['sems']):
                cnt = (d['n'] - i + self.NS - 1) // self.NS if d['n'] > i else 0
                if cnt > 0:
                    evs[s.num] = (s, 16 * cnt)
        for e in self.engs:
            self._wait(e, evs)
        self.buf.clear()

    def _replay(self, e, eng):
        for it in self.prog[e]:
            if it[0] == 'w':
                eng.wait_ge(it[1], it[2])
            elif it[0] == 'i':
                ins = it[1]()
                if it[2] is not None:
                    ins.then_inc(it[2], 1)
            else:
                eng.dma_start(out=it[1], in_=it[2], **it[3]).then_inc(it[4], 16)

    def finish(self):
        self.barrier()
        with self.nc.Block() as block:
            @block.sync
            def _(eng):
                self._replay('sp', eng)

            @block.tensor
            def _(eng):
                self._replay('pe', eng)

            @block.vector
            def _(eng):
                self._replay('dve', eng)

            @block.scalar
            def _(eng):
                self._replay('act', eng)

            @block.gpsimd
            def _(eng):
                self._replay('pool', eng)


class KB:
    def __init__(self, nc, debug=()):
        self.nc = nc
        self.debug = set(debug)
        self.fw = FW(nc)
        self.arena_bytes = 212000
        ar = nc.alloc_sbuf_tensor("arena", [128, self.arena_bytes], U8)
        self.base = nc.lookup_mloc(ar).addr
        self.pers = 0
        self.top = self.arena_bytes
        self.off = 0
        self.uid = 0
        self.psall = nc.alloc_psum_tensor("psall", [128, 8, 512], F32)
        self.ps = [self.psall[:, i, :] for i in range(8)]

    def sb(self, name, shape, dtype, persistent=False):
        nb = int(np.prod(shape[1:])) * (4 if dtype == F32 else 2)
        nb = (nb + 63) // 64 * 64
        self.uid += 1
        t = self.nc.alloc_sbuf_tensor_at(f"{name}_{self.uid}", list(shape), dtype, offset=self.base + self.off)
        self.off += nb
        assert self.off <= self.top, (name, self.off, self.top)
        if persistent:
            self.pers = self.off
        return t

    def sb_top(self, name, shape, dtype):
        nb = int(np.prod(shape[1:])) * (4 if dtype == F32 else 2)
        nb = (nb + 63) // 64 * 64
        self.top -= nb
        assert self.top >= self.off, (name, self.top, self.off)
        self.uid += 1
        return self.nc.alloc_sbuf_tensor_at(f"{name}_{self.uid}", list(shape), dtype, offset=self.base + self.top)

    def phase(self):
        self.fw.barrier()
        self.off = self.pers
        if getattr(self, 'top_release', False):
            self.top = self.arena_bytes
            self.top_release = False

    def dram(self, name, shape, dtype):
        kind = "ExternalOutput" if name in self.debug else "Internal"
        return self.nc.dram_tensor(name, list(shape), dtype, kind=kind).ap()


def act_engine_copy(kb, eng, out, in_):
    nc = kb.nc
    if eng == 'act':
        return lambda: nc.scalar.copy(out=out, in_=in_)
    if eng == 'dve':
        return lambda: nc.vector.tensor_copy(out=out, in_=in_)
    return lambda: nc.gpsimd.tensor_copy(out=out, in_=in_)


def ln_group(kb, tag, xt, kx, n, xh, kxh, st, mv, rs, eps):
    nc, fw = kb.nc, kb.fw
    for i in range(n):
        for j in range(2):
            fw.op('dve', lambda i=i, j=j: nc.vector.bn_stats(out=st[:, i, j, :], in_=xt[:, i, j * 512:(j + 1) * 512]),
                  reads=[(kx, i)], writes=[(tag + 'st', i, j)])
        fw.op('dve', lambda i=i: nc.vector.bn_aggr(out=mv[:, i, :], in_=st[:, i, :, :].rearrange("p a b -> p (a b)")),
              reads=[(tag + 'st', i, 0), (tag + 'st', i, 1)], writes=[(tag + 'mv', i)])
    fw.op('act', lambda: nc.scalar.activation(out=rs[:, 0:n], in_=mv[:, 0:n, 1], func=AF.Ln, bias=eps[:], scale=1.0),
          reads=[(tag + 'mv', i) for i in range(n)], writes=[tag + 'rs'])
    fw.op('act', lambda: nc.scalar.activation(out=rs[:, 0:n], in_=rs[:, 0:n], func=AF.Exp, scale=-0.5),
          reads=[tag + 'rs'], writes=[tag + 'rs'])
    for i in range(n):
        fw.op('dve', lambda i=i: nc.vector.tensor_scalar(out=xh[:, i, :], in0=xt[:, i, :], scalar1=mv[:, i, 0:1], scalar2=rs[:, i:i + 1],
                                                          op0=ALU.subtract, op1=ALU.mult),
              reads=[(kx, i), (tag + 'mv', i), tag + 'rs'], writes=[(kxh, i)])


def transpose_mod(kb, tag, xh, kxh, n, tok0, uT, kuT, cst, lidx, jshift, jscale, pbanks):
    nc, fw = kb.nc, kb.fw
    for i in range(n):
        nsel = 0 if (tok0 + i * 128) < NL else 1
        for g in range(2):
            pb = pbanks[(i * 2 + g) % len(pbanks)]
            pt = kb.ps[pb]
            for j in range(4):
                k = g * 4 + j
                fw.op('pe', lambda i=i, k=k, j=j, pt=pt: nc.tensor.transpose(out=pt[:, j * 128:(j + 1) * 128], in_=xh[:, i, k * 128:(k + 1) * 128], identity=cst['ident'][:]),
                      reads=[(kxh, i)], writes=[('ps', pb)])
            for j in range(4):
                k = g * 4 + j
                fw.op('act', lambda i=i, k=k, j=j, pt=pt, nsel=nsel: nc.scalar.activation(
                    out=uT[:, k, i * 128:(i + 1) * 128], in_=pt[:, j * 128:(j + 1) * 128], func=AF.Identity,
                    scale=cst['mod1p'][:, lidx, jscale * 8 + k, nsel:nsel + 1], bias=cst['modT'][:, lidx, jshift * 8 + k, nsel:nsel + 1]),
                    reads=[('ps', pb)], writes=[(kuT, i, k)])


def phase_consts(kb, io):
    nc, fw = kb.nc, kb.fw
    cst = {}
    cst['ident'] = kb.sb('ident', [128, 128], F32, True)
    cst['ones'] = kb.sb('ones', [128, 128], F32, True)
    cst['eps'] = kb.sb('eps', [128, 1], F32, True)
    cst['modT'] = kb.sb('modT', [128, 2, 48, 2], F32, True)
    cst['mod1p'] = kb.sb('mod1p', [128, 2, 48, 2], F32, True)
    fw.dma('sp', cst['ident'][:], io['ident'], writes=['ident'])
    fw.op('pool', lambda: nc.gpsimd.memset(cst['ones'][:], 1.0), writes=['ones'])
    fw.op('pool', lambda: nc.gpsimd.memset(cst['eps'][:], LN_EPS), writes=['eps'])
    return cst


def phase_mods(kb, io, cst, S):
    nc, fw = kb.nc, kb.fw
    modT, mod1p = cst['modT'], cst['mod1p']
    cv = kb.sb('cv', [128, 8, 2], F32)
    sv = kb.sb('sv', [128, 8, 2], F32)
    bT = kb.sb('bT', [128, 2, 48], F32)
    wm = [kb.sb(f'wm{i}', [128, 8, 512], F32) for i in range(4)]
    fw.dma('sp', cv[:], io['cvec'], writes=['cv'])
    fw.dma('sp', bT[:], io['b_modT'], writes=['bT'])
    fw.op('act', lambda: nc.scalar.activation(out=sv[:], in_=cv[:], func=AF.Silu), reads=['cv'], writes=['sv'])
    it = 0
    for l in range(2):
        wv = io['w_mod'][l].rearrange("(k p) n -> p k n", p=128)
        for cb in range(12):
            b = it % 4
            fw.dma(('sp', 'act', 'pool')[it % 3], wm[b][:], wv[:, :, cb * 512:(cb + 1) * 512], writes=[('wm', b)])
            pb = it % 2
            for ci in range(4):
                for k in range(8):
                    fw.op('pe', lambda b=b, ci=ci, k=k, pb=pb: nc.tensor.matmul(kb.ps[pb][:, ci * 2:ci * 2 + 2], lhsT=wm[b][:, k, ci * 128:(ci + 1) * 128],
                                                                                rhs=sv[:, k, :], start=(k == 0), stop=(k == 7)),
                          reads=[('wm', b), 'sv'], writes=[('ps', pb)], signal=(k == 7))
            for n in range(2):
                fw.op('dve', lambda l=l, cb=cb, n=n, pb=pb: nc.vector.tensor_tensor(
                    out=modT[:, l, cb * 4:(cb + 1) * 4, n], in0=kb.ps[pb][:, 0:8].rearrange("p (c n) -> p c n", n=2)[:, :, n],
                    in1=bT[:, l, cb * 4:(cb + 1) * 4], op=ALU.add),
                    reads=[('ps', pb), 'bT'], writes=[('modT', l, cb, n)])
            it += 1
    allmod = [('modT', l, cb, n) for l in range(2) for cb in range(12) for n in range(2)]
    fw.op('dve', lambda: nc.vector.tensor_scalar_add(out=mod1p[:], in0=modT[:], scalar1=1.0), reads=allmod, writes=['mod1p'])
    dg = [kb.sb(f'dg{i}', [128, 128], F32) for i in range(2)]
    gbt = [kb.sb(f'gbt{i}', [128, 1024], F32) for i in range(2)]
    it = 0
    gi = 0
    for l in range(2):
        for jj, j in enumerate((2, 5)):
            for n in range(2):
                g = gi % 2
                for k in range(8):
                    b = it % 2
                    pb = 2 + (it // 4) % 2
                    fw.op('dve', lambda b=b, l=l, j=j, k=k, n=n: nc.vector.tensor_scalar_mul(out=dg[b][:], in0=cst['ident'][:], scalar1=modT[:, l, j * 8 + k, n:n + 1]),
                          reads=['ident', 'mod1p'], writes=[('dg', b)])
                    fw.op('pe', lambda b=b, pb=pb, k=k: nc.tensor.matmul(kb.ps[pb][:, (k % 4) * 128:(k % 4 + 1) * 128], lhsT=cst['ones'][:], rhs=dg[b][:], start=True, stop=True),
                          reads=[('dg', b), 'ones'], writes=[('ps', pb)])
                    if k % 4 == 3:
                        fw.op('act', lambda g=g, pb=pb, k=k: nc.scalar.copy(out=gbt[g][:, (k // 4) * 512:(k // 4 + 1) * 512], in_=kb.ps[pb][:]),
                              reads=[('ps', pb)], writes=[('gbt', g, k // 4)])
                    it += 1
                fw.dma('sp', S['gb'][l, jj, n], gbt[g][:], reads=[('gbt', g, 0), ('gbt', g, 1)])
                gi += 1


def phase_A(kb, io, cst, S):
    nc, fw = kb.nc, kb.fw
    W = kb.sb('Wa', [128, 8, 4096], BF16)
    wv = io['a_win'].rearrange("(k p) n -> p k n", p=128)
    for k in range(8):
        fw.dma('pool', W[:, k, :], wv[:, k, :], writes=[('Wa', k)])
    Wk = [('Wa', k) for k in range(8)]
    xt = [kb.sb(f'xt{i}', [128, 4, 1024], F32) for i in range(2)]
    uT = [kb.sb(f'uT{i}', [128, 8, 512], BF16) for i in range(2)]
    rc = [kb.sb(f'rc{i}', [128, 512], F32) for i in range(2)]
    rs_ = [kb.sb(f'rs{i}', [128, 512], F32) for i in range(2)]
    st = kb.sb('st', [128, 4, 2, 6], F32)
    mv = kb.sb('mv', [128, 4, 2], F32)
    rstd = kb.sb('rstd', [128, 4], F32)
    t1 = [kb.sb(f't1{i}', [128, 512], F32) for i in range(2)]
    t2 = [kb.sb(f't2{i}', [128, 512], F32) for i in range(2)]
    ob = [kb.sb(f'ob{i}', [128, 512], BF16) for i in range(4)]
    vt = [kb.sb(f'vt{i}', [128, 768], BF16) for i in range(2)]
    ngroups = (NT + 511) // 512
    obi = 0
    pbi = 0
    vti = 0
    for g in range(ngroups):
        tok0 = g * 512
        ntok = min(512, NT - tok0)
        n = ntok // 128
        b = g % 2
        for i in range(n):
            fw.dma('sp', xt[b][:, i, :], io['xin'][tok0 + i * 128: tok0 + (i + 1) * 128, :], writes=[(('xt', b), i)])
        fw.dma('sp', rc[b][:, 0:ntok], io['ropeC'][:, tok0:tok0 + ntok], writes=[('rc', b)])
        fw.dma('sp', rs_[b][:, 0:ntok], io['ropeS'][:, tok0:tok0 + ntok], writes=[('rs', b)])
        ln_group(kb, 'A', xt[b], ('xt', b), n, xt[b], ('xt', b), st, mv, rstd, cst['eps'])
        transpose_mod(kb, 'A', xt[b], ('xt', b), n, tok0, uT[b], ('uT', b), cst, 0, 0, 1, (0, 1))
        uk = [(('uT', b), i, k) for i in range(n) for k in range(8)]
        for kind in range(2):
            for h in range(6):
                pa = 2 + pbi % 6
                pbi += 1
                pbb = 2 + pbi % 6
                pbi += 1
                c0 = kind * 1536 + h * 128
                for k in range(8):
                    fw.op('pe', lambda k=k, pa=pa, c0=c0, b=b, ntok=ntok: nc.tensor.matmul(kb.ps[pa][:, 0:ntok], lhsT=W[:, k, c0:c0 + 128], rhs=uT[b][:, k, 0:ntok], start=(k == 0), stop=(k == 7)),
                          reads=uk + Wk if k == 0 else [], writes=[('ps', pa)], signal=(k == 7))
                for k in range(8):
                    fw.op('pe', lambda k=k, pbb=pbb, c0=c0, b=b, ntok=ntok: nc.tensor.matmul(kb.ps[pbb][:, 0:ntok], lhsT=W[:, k, c0 + 768:c0 + 896], rhs=uT[b][:, k, 0:ntok], start=(k == 0), stop=(k == 7)),
                          reads=[], writes=[('ps', pbb)], signal=(k == 7))
                tb = obi % 2
                o = obi % 4
                obi += 1
                fw.op('dve', lambda pa=pa, tb=tb, b=b, ntok=ntok: nc.vector.tensor_tensor(out=t1[tb][:, 0:ntok], in0=kb.ps[pa][:, 0:ntok], in1=rc[b][:, 0:ntok], op=ALU.mult),
                      reads=[('ps', pa), ('rc', b)], writes=[('t1', tb)])
                fw.op('dve', lambda pbb=pbb, tb=tb, b=b, ntok=ntok: nc.vector.tensor_tensor(out=t2[tb][:, 0:ntok], in0=kb.ps[pbb][:, 0:ntok], in1=rs_[b][:, 0:ntok], op=ALU.mult),
                      reads=[('ps', pbb), ('rs', b)], writes=[('t2', tb)])
                fw.op('pool', lambda tb=tb, o=o, ntok=ntok: nc.gpsimd.tensor_tensor(out=ob[o][:, 0:ntok], in0=t1[tb][:, 0:ntok], in1=t2[tb][:, 0:ntok], op=ALU.add),
                      reads=[('t1', tb), ('t2', tb)], writes=[('ob', o)])
                dst = S['qT'] if kind == 0 else S['kT']
                fw.dma('sp', dst[h, :, tok0:tok0 + ntok], ob[o][:, 0:ntok], reads=[('ob', o)])
        for j in range(2):
            pa = 2 + pbi % 6
            pbi += 1
            c0 = 3072 + j * 128
            for k in range(8):
                fw.op('pe', lambda k=k, pa=pa, c0=c0, b=b, ntok=ntok: nc.tensor.matmul(kb.ps[pa][:, 0:ntok], lhsT=W[:, k, c0:c0 + 128], rhs=uT[b][:, k, 0:ntok], start=(k == 0), stop=(k == 7)),
                      reads=uk + Wk if k == 0 else [], writes=[('ps', pa)], signal=(k == 7))
            o = obi % 4
            obi += 1
            fw.op('act', lambda pa=pa, o=o, ntok=ntok: nc.scalar.copy(out=ob[o][:, 0:ntok], in_=kb.ps[pa][:, 0:ntok]), reads=[('ps', pa)], writes=[('ob', o)])
            fw.dma('sp', S['fT'][j * 128:(j + 1) * 128, tok0:tok0 + ntok], ob[o][:, 0:ntok], reads=[('ob', o)])
        for i in range(n):
            v = vti % 2
            vti += 1
            for half in range(2):
                pa = 2 + pbi % 6
                pbi += 1
                c0 = 3328 + half * 384
                for k in range(8):
                    fw.op('pe', lambda k=k, pa=pa, c0=c0, b=b, i=i: nc.tensor.matmul(kb.ps[pa][:, 0:384], lhsT=uT[b][:, k, i * 128:(i + 1) * 128], rhs=W[:, k, c0:c0 + 384], start=(k == 0), stop=(k == 7)),
                          reads=uk + Wk if k == 0 else [], writes=[('ps', pa)], signal=(k == 7))
                fw.op('act', lambda pa=pa, v=v, half=half: nc.scalar.copy(out=vt[v][:, half * 384:(half + 1) * 384], in_=kb.ps[pa][:, 0:384]),
                      reads=[('ps', pa)], writes=[('vt', v, half)])
            fw.dma('sp', S['v'][tok0 + i * 128: tok0 + (i + 1) * 128, :], vt[v][:], reads=[('vt', v, 0), ('vt', v, 1)])


IN_SPECS = dict(
    xin=([NT, D], F32), cvec=([128, 8, 2], F32), w_mod=([2, D, 6 * D], F32), b_modT=([128, 2, 48], F32),
    ln_g=([2, 2, D], F32), ln_b=([2, 2, D], F32), w_ff1=([2, D, DFF], F32), w_ff3=([2, D, DFF], F32), w_ff2=([2, DFF, D], F32),
    a_win=([D, 4096], F32), a_wout=([D, D], F32), lamv=([4, 64], F32), da_hg=([128], F32),
    ropeC=([128, NT], F32), ropeS=([128, NT], F32), ident=([128, 128], F32), tri=([2, 128, 128], F32),
    bdc=([2, 2, 128, 512], BF16), dftN=([NL, 2, NL], BF16), dftC=([NCX, 2, NCX], BF16),
    m_win=([D, 4096], F32), m_wout=([2048, D], F32), convw=([128, 16, 5], F32), convb=([128, 16], F32),
    bdq=([16, 128, 128], F32), bdk=([16, 128, 128], F32), bdv=([16, 128, 128], F32),
    wg=([6144, 32], F32), bg=([32], F32), skipT=([128, 16], F32), mhgT=([128, 16], F32),
)


def build(stop=99, debug=(), only=None):
    nc = bass.Bass("TRN2", target_bir_lowering=False)
    io = {k: nc.dram_tensor(k, list(sh), dt, kind="ExternalInput").ap() for k, (sh, dt) in IN_SPECS.items()}
    out = nc.dram_tensor("out", [NL, D], F32, kind="ExternalOutput").ap()
    kb = KB(nc, debug)
    S = {}
    S['gb'] = kb.dram('gb', [2, 2, 2, 128, D], F32)
    S['qT'] = kb.dram('qT', [6, 128, NT], BF16)
    S['kT'] = kb.dram('kT', [6, 128, NT], BF16)
    S['v'] = kb.dram('v', [NT, 768], BF16)
    S['fT'] = kb.dram('fT', [256, NT], BF16)
    S['an'] = kb.dram('an', [NT, 768], F32)
    S['yT'] = kb.dram('yT', [256, NT], BF16)
    S['h1'] = kb.dram('h1', [NT, D], F32)
    if 'h2in' in debug:
        S['h2'] = nc.dram_tensor('h2in', [NT, D], F32, kind="ExternalInput").ap()
    else:
        S['h2'] = kb.dram('h2', [NT, D], F32)
    S['xmT'] = kb.dram('xmT', [2048, NT], BF16)
    S['szT'] = kb.dram('szT', [2048, NL], BF16)
    S['qT1'] = kb.dram('qT1', [2048, NT], BF16)
    S['kT1'] = kb.dram('kT1', [2048, NT], BF16)
    S['xcT'] = kb.dram('xcT', [2048, NL], BF16)
    S['ktok'] = kb.dram('ktok', [NT, 2048], BF16)
    S['vtok'] = kb.dram('vtok', [NT, 2048], BF16)
    S['gT'] = kb.dram('gT', [32, NT], F32)
    S['yT1'] = kb.dram('yT1', [2048, NL], BF16)
    cst = phase_consts(kb, io)
    T = {}

    def ph_M3(kb, io, cst, S):
        T['pers0'] = kb.pers
        for nm in ('EQ', 'EKS', 'E', 'EKW'):
            T[nm] = kb.sb(nm, [128, NTILE, 16], F32, True)
        phase_M3(kb, io, cst, S, T)

    def ph_M4(kb, io, cst, S):
        if 'tabs' in debug:
            tabs = nc.dram_tensor("tabs", [4, 128, NTILE * 16], F32, kind="ExternalOutput").ap()
            for i, nm in enumerate(('EQ', 'EKS', 'E', 'EKW')):
                kb.fw.dma('sp', tabs[i], T[nm][:].rearrange("p a b -> p (a b)"), reads=[])
        phase_M4(kb, io, cst, S, T)

    def ph_D1b(kb, io, cst, S):
        kb.pers = T.get('pers0', kb.pers)
        kb.off = kb.pers
        phase_D1(kb, io, cst, S, 1)
    phases = [phase_mods, phase_A, phase_B, phase_C,
              lambda kb, io, cst, S: phase_D1(kb, io, cst, S, 0),
              lambda kb, io, cst, S: phase_D2(kb, io, cst, S, 0, S['h2']),
              phase_M1, phase_M2, ph_M3, ph_M4, ph_D1b,
              lambda kb, io, cst, S: phase_D2(kb, io, cst, S, 1, out)]
    order = list(only) if only is not None else list(range(min(stop, len(phases))))
    for i in order:
        kb.phase()
        phases[i](kb, io, cst, S)
    if 'modT' in debug:
        dm = nc.dram_tensor("modT_o", [128, 2 * 48 * 2], F32, kind="ExternalOutput").ap()
        kb.fw.dma('sp', dm, cst['modT'][:].rearrange("p a b c -> p (a b c)"), reads=[])
    kb.fw.finish()
    return nc


def rope_tables():
    rows = NL // 64
    row = np.repeat(np.arange(rows, dtype=np.float32), 64)
    col = np.tile(np.arange(64, dtype=np.float32), rows)
    inv_freq = (np.float32(10000.0) ** (-np.arange(16, dtype=np.float32) / np.float32(16))).astype(np.float32)
    ang = np.concatenate([row[:, None] * inv_freq, col[:, None] * inv_freq], -1).astype(np.float32)
    cs, sn = np.cos(ang).astype(np.float32), np.sin(ang).astype(np.float32)
    C = np.ones((128, NT), np.float32)
    Sg = np.zeros((128, NT), np.float32)
    for m in range(2):
        for half in range(2):
            p0 = m * 64 + half * 32
            C[p0:p0 + 32, :NL] = cs.T
            Sg[p0:p0 + 32, :NL] = (-sn.T if half == 0 else sn.T)
    return C, Sg


def dft_tables():
    def dft(n):
        idx = (np.arange(n, dtype=np.int64)[:, None] * np.arange(n, dtype=np.int64)[None, :]) % n
        ang = 2.0 * np.pi * idx.astype(np.float64) / n
        return np.cos(ang), np.sin(ang)
    bf = ml_dtypes.bfloat16
    cN, sN = dft(NL)
    dftN = np.stack([cN, -sN], 1).astype(bf)
    cC, sC = dft(NCX)
    dftC = np.stack([cC, -sC], 1).astype(bf)
    c64, s64 = dft(64)
    bdc = np.zeros((2, 2, 128, 2, 4, 64), np.float64)
    for v, ntok in enumerate((NL, NCX)):
        nrm = 1.0 / math.sqrt(ntok * 64)
        for j in range(2):
            for g2 in range(2):
                g = 2 * j + g2
                bdc[v, j, g2 * 64:(g2 + 1) * 64, 0, g, :] = c64 * nrm
                bdc[v, j, g2 * 64:(g2 + 1) * 64, 1, g, :] = s64 * nrm
    return dftN, dftC, bdc.reshape(2, 2, 128, 512).astype(bf)


_CONST_CACHE = {}


def prep_shared(inp):
    f32 = np.float32
    sh = {}
    sh['w_mod'] = np.ascontiguousarray(inp['w_mod'], f32)
    sh['b_modT'] = np.ascontiguousarray(inp['b_mod'].reshape(2, 48, 128).transpose(2, 0, 1), f32)
    for k in ('ln_g', 'ln_b', 'w_ff1', 'w_ff3', 'w_ff2'):
        sh[k] = np.ascontiguousarray(inp[k], f32)
    w = inp['a_w_in'][0]
    cols = {k: [] for k in ('q', 'qs', 'k', 'ks', 'v')}
    for h in range(6):
        b0 = h * 384
        for m in range(2):
            q0 = b0 + m * 64
            k0 = b0 + 128 + m * 64
            cols['q'] += list(range(q0, q0 + 64))
            cols['qs'] += list(range(q0 + 32, q0 + 64)) + list(range(q0, q0 + 32))
            cols['k'] += list(range(k0, k0 + 64))
            cols['ks'] += list(range(k0 + 32, k0 + 64)) + list(range(k0, k0 + 32))
        cols['v'] += list(range(b0 + 256, b0 + 384))
    order = cols['q'] + cols['qs'] + cols['k'] + cols['ks'] + list(range(2304, 2560)) + cols['v']
    sh['a_win'] = np.ascontiguousarray(w[:, order], f32)
    sh['a_wout'] = np.ascontiguousarray(inp['a_w_out'][0], f32)
    sh['lamv'] = np.ascontiguousarray(np.stack([inp['da_lq1'][0], inp['da_lk1'][0], inp['da_lq2'][0], inp['da_lk2'][0]]), f32)
    sh['da_hg'] = np.ascontiguousarray(inp['da_head_g'][0], f32)
    if not _CONST_CACHE:
        C, Sg = rope_tables()
        dftN, dftC, bdc = dft_tables()
        tri = np.stack([np.triu(np.ones((128, 128), f32)), np.tril(np.ones((128, 128), f32))])
        _CONST_CACHE.update(ropeC=C, ropeS=Sg, dftN=dftN, dftC=dftC, bdc=bdc, tri=tri, ident=np.eye(128, dtype=f32))
    sh.update(_CONST_CACHE)
    sh['m_win'] = np.ascontiguousarray(inp['m_w_in'][0], f32)
    sh['m_wout'] = np.ascontiguousarray(inp['m_w_out'][0], f32)
    sh['convw'] = np.ascontiguousarray(inp['m_conv_w'][0].reshape(5, 16, 128).transpose(2, 1, 0), f32)
    sh['convb'] = np.ascontiguousarray(inp['m_conv_b'][0].reshape(16, 128).T, f32)
    for nm, key in (('bdq', 'm_wq'), ('bdk', 'm_wk'), ('bdv', 'm_wv')):
        bd = np.zeros((16, 128, 128), f32)
        blk = inp[key][0].reshape(16, 32, 4, 4)
        for j in range(32):
            bd[:, 4 * j:4 * j + 4, 4 * j:4 * j + 4] = blk[:, j]
        sh[nm] = bd
    wg = np.concatenate([inp['m_w_ig'][0].transpose(1, 0, 2).reshape(6144, 16), inp['m_w_fg'][0].transpose(1, 0, 2).reshape(6144, 16)], 1)
    sh['wg'] = np.ascontiguousarray(wg, f32)
    sh['bg'] = np.ascontiguousarray(np.concatenate([inp['m_b_ig'][0].reshape(16), inp['m_b_fg'][0].reshape(16)]), f32)
    sh['skipT'] = np.ascontiguousarray(inp['m_skip'][0].reshape(16, 128).T, f32)
    sh['mhgT'] = np.ascontiguousarray(inp['m_head_g'][0].reshape(16, 128).T, f32)
    return sh


def prep_core(inp, b):
    f32 = np.float32
    xin = np.concatenate([inp['x'][b], inp['ctx'][b]], 0).astype(f32)
    cv = np.stack([inp['c'][b], inp['c_ctx']], 1).astype(f32)
    cvec = np.ascontiguousarray(cv.reshape(8, 128, 2).transpose(1, 0, 2))
    return dict(xin=np.ascontiguousarray(xin), cvec=cvec)


def phase_B(kb, io, cst, S):
    nc, fw = kb.nc, kb.fw
    lv = kb.sb('lv', [128, 4, 64], F32)
    pr = kb.sb('pr', [128, 2, 64], F32)
    a12 = kb.sb('a12', [128, 2], F32)
    lam = kb.sb('lam', [128, 1], F32)
    nlam = kb.sb('nlam', [128, 1], F32)
    fw.dma('sp', lv[:].rearrange("p a b -> p (a b)"), io['lamv'].rearrange("a b -> (a b)").partition_broadcast(128), writes=['lv'])
    fw.op('dve', lambda: nc.vector.tensor_tensor(out=pr[:, 0, :], in0=lv[:, 0, :], in1=lv[:, 1, :], op=ALU.mult), reads=['lv'], writes=[('pr', 0)])
    fw.op('dve', lambda: nc.vector.tensor_tensor(out=pr[:, 1, :], in0=lv[:, 2, :], in1=lv[:, 3, :], op=ALU.mult), reads=['lv'], writes=[('pr', 1)])
    fw.op('dve', lambda: nc.vector.reduce_sum(out=a12[:], in_=pr[:], axis=mybir.AxisListType.X), reads=[('pr', 0), ('pr', 1)], writes=['a12'])
    fw.op('act', lambda: nc.scalar.activation(out=a12[:], in_=a12[:], func=AF.Exp), reads=['a12'], writes=['a12'])
    fw.op('dve', lambda: nc.vector.tensor_tensor(out=lam[:], in0=a12[:, 0:1], in1=a12[:, 1:2], op=ALU.subtract), reads=['a12'], writes=['lam'])
    fw.op('dve', lambda: nc.vector.tensor_scalar(out=nlam[:], in0=lam[:], scalar1=LAM_INIT, scalar2=-1.0, op0=ALU.add, op1=ALU.mult), reads=['lam'], writes=['nlam'])

    qT = [kb.sb(f'qTh{i}', [128, NT], BF16) for i in range(2)]
    kT = [kb.sb(f'kTh{i}', [128, NT], BF16) for i in range(2)]
    V = [kb.sb(f'Vh{i}', [128, NTILE, 129], BF16) for i in range(2)]
    pt = [kb.sb(f'pt{i}', [128, 2, 512], BF16) for i in range(3)]
    rr = [kb.sb(f'rr{i}', [128, 4], F32) for i in range(2)]
    tt = [kb.sb(f'tt{i}', [128, 128], F32) for i in range(2)]
    ot = [kb.sb(f'ot{i}', [128, 4, 128], F32) for i in range(2)]
    for i in range(2):
        fw.op('pool', lambda i=i: nc.gpsimd.memset(V[i][:], 1.0), writes=[('V', i)])
    its = []
    for h in range(6):
        for qb in range(9):
            q0 = qb * 512
            nq = min(512, NT - q0)
            keys = list(range(NTILE)) if qb < 8 else [32, 33]
            for kt in keys:
                its.append(dict(h=h, hb=h % 2, qb=qb, q0=q0, nq=nq, nqi=nq // 128, kt=kt, first=(kt == keys[0]), last=(kt == keys[-1])))

    def loads(h):
        hb = h % 2
        fw.dma('sp', qT[hb][:], S['qT'][h], writes=[('qT', hb)])
        fw.dma('sp', kT[hb][:], S['kT'][h], writes=[('kT', hb)])
        fw.dma('pool', V[hb][:, :, 0:128], S['v'][:, h * 128:(h + 1) * 128].rearrange("(t p) c -> p t c", p=128), writes=[('V', hb)])

    def qk(it, n):
        pp = n % 2
        hb, kt, q0, nq = it['hb'], it['kt'], it['q0'], it['nq']
        for m in range(2):
            fw.op('pe', lambda m=m: nc.tensor.matmul(
                kb.ps[2 * pp + m][:, 0:nq], lhsT=kT[hb][m * 64:(m + 1) * 64, kt * 128:(kt + 1) * 128], rhs=qT[hb][m * 64:(m + 1) * 64, q0:q0 + nq], start=True, stop=True),
                reads=[('qT', hb), ('kT', hb)], writes=[('ps', 2 * pp + m)], signal=(m == 1))

    def ex(it, n):
        pp = n % 2
        pi = n % 3
        nq = it['nq']
        fw.op('act', lambda: nc.scalar.activation(out=pt[pi][:, :, 0:nq], in_=kb.psall[:, 2 * pp:2 * pp + 2, 0:nq], func=AF.Exp, scale=0.125),
              reads=[('ps', 2 * pp), ('ps', 2 * pp + 1)], writes=[('pt', pi)])

    def av(it, n):
        pi = n % 3
        hb, kt, nqi = it['hb'], it['kt'], it['nqi']
        for m in range(2):
            for qi in range(nqi):
                bank = 4 + m * 2 + qi // 2
                c0 = (qi % 2) * 256
                last = (m == 1 and qi == nqi - 1)
                fw.op('pe', lambda m=m, qi=qi, bank=bank, c0=c0: nc.tensor.matmul(
                    kb.ps[bank][:, c0:c0 + 129], lhsT=pt[pi][:, m, qi * 128:(qi + 1) * 128], rhs=V[hb][:, kt, :],
                    start=(it['first'] and qi % 2 == 0), stop=it['last'], skip_group_check=True),
                    reads=[('pt', pi), ('V', hb)], writes=[('ps', bank)], signal=last)

    blkc = [0]

    accs = [kb.sb(f'accs{i}', [128, 4, 512], F32) for i in range(2)]

    def epi(it):
        ob = blkc[0] % 2
        blkc[0] += 1
        h, q0, nq, nqi = it['h'], it['q0'], it['nq'], it['nqi']
        nbk = (nqi + 1) // 2
        for m in range(2):
            for bb in range(nbk):
                bank = 4 + m * 2 + bb
                fw.op('dve', lambda m=m, bb=bb, bank=bank: nc.vector.tensor_copy(out=accs[ob][:, m * 2 + bb, :], in_=kb.ps[bank][:]), reads=[('ps', bank)], writes=[('accs', ob, m * 2 + bb)])
        for qi in range(nqi):
            a0 = qi // 2
            a1 = 2 + qi // 2
            c0 = (qi % 2) * 256
            tb = qi % 2
            fw.op('dve', lambda a0=a0, a1=a1, c0=c0, tb=tb, qi=qi: nc.vector.reciprocal(out=rr[ob][:, 0:1], in_=accs[ob][:, a0, c0 + 128:c0 + 129]), reads=[('accs', ob, a0)], writes=[('rr', ob, 0)])
            fw.op('dve', lambda a0=a0, a1=a1, c0=c0, tb=tb, qi=qi: nc.vector.reciprocal(out=rr[ob][:, 1:2], in_=accs[ob][:, a1, c0 + 128:c0 + 129]), reads=[('accs', ob, a1)], writes=[('rr', ob, 1)])
            fw.op('dve', lambda a0=a0, a1=a1, c0=c0, tb=tb, qi=qi: nc.vector.tensor_tensor(out=rr[ob][:, 2:3], in0=rr[ob][:, 1:2], in1=nlam[:], op=ALU.mult), reads=[('rr', ob, 1), 'nlam'], writes=[('rr', ob, 2)])
            fw.op('dve', lambda a0=a0, a1=a1, c0=c0, tb=tb, qi=qi: nc.vector.tensor_scalar_mul(out=tt[tb][:], in0=accs[ob][:, a0, c0:c0 + 128], scalar1=rr[ob][:, 0:1]),
                  reads=[('accs', ob, a0), ('rr', ob, 0)], writes=[('tt', tb)])
            fw.op('dve', lambda a0=a0, a1=a1, c0=c0, tb=tb, qi=qi: nc.vector.scalar_tensor_tensor(out=ot[ob][:, qi, :], in0=accs[ob][:, a1, c0:c0 + 128], scalar=rr[ob][:, 2:3], in1=tt[tb][:], op0=ALU.mult, op1=ALU.add),
                  reads=[('accs', ob, a1), ('rr', ob, 2), ('tt', tb)], writes=[('ot', ob, qi)])
        fw.dma('sp', S['an'][q0:q0 + nq, h * 128:(h + 1) * 128].rearrange("(t p) c -> p t c", p=128), ot[ob][:, 0:nqi, :],
               reads=[('ot', ob, qi) for qi in range(nqi)])

    loads(0)
    qk(its[0], 0)
    for n, it in enumerate(its):
        if it['qb'] == 0 and it['first'] and it['h'] + 1 < 6:
            loads(it['h'] + 1)
        if n + 1 < len(its):
            qk(its[n + 1], n + 1)
        ex(it, n)
        av(it, n)
        if it['last']:
            epi(it)


def phase_C(kb, io, cst, S):
    nc, fw = kb.nc, kb.fw
    fTt = kb.sb('fTt', [128, 2, NT], BF16)
    bdc = kb.sb('bdc', [128, 2, 2, 512], BF16)
    Gt = kb.sb('Gt', [128, NTILE, 512], BF16)
    yTt = kb.sb('yTt', [128, 2, NT], BF16)
    dn = [kb.sb(f'dn{i}', [128, 2, 2048], BF16) for i in range(6)]
    dnc = kb.sb('dnc', [128, 2, 2, NCX], BF16)
    for j in range(2):
        fw.dma('sp', fTt[:, j, :], S['fT'][j * 128:(j + 1) * 128, :], writes=[('fTt', j)])
    fw.dma('sp', bdc[:], io['bdc'].rearrange("v j p n -> p v j n"), writes=['bdc'])
    fw.dma('sp', dnc[:], io['dftC'].rearrange("(t p) c n -> p t c n", p=128), writes=['dnc'])
    for t in range(NTILE):
        v = 0 if t < 32 else 1
        pb = t % 4
        for j in range(2):
            fw.op('pe', lambda t=t, j=j, v=v, pb=pb: nc.tensor.matmul(kb.ps[pb][:], lhsT=fTt[:, j, t * 128:(t + 1) * 128], rhs=bdc[:, v, j, :], start=(j == 0), stop=(j == 1)),
                  reads=[('fTt', 0), ('fTt', 1), 'bdc'], writes=[('ps', pb)], signal=(j == 1))
        fw.op('act' if t % 2 == 0 else 'dve', act_engine_copy(kb, 'act' if t % 2 == 0 else 'dve', Gt[:, t, :], kb.ps[pb][:]), reads=[('ps', pb)], writes=[('Gt', t)])
    kb.fw.barrier()
    di = 0
    for p in range(2):
        for t in range(32):
            b = di % 6
            di += 1
            fw.dma(('sp', 'pool', 'act')[di % 3], dn[b][:], io['dftN'][t * 128:(t + 1) * 128, :, p * 2048:(p + 1) * 2048], writes=[('dn', b)])
            cnt = 0
            for cs in range(2):
                for jj in range(2):
                    for nb in range(4):
                        cnt += 1
                        fw.op('pe', lambda t=t, cs=cs, jj=jj, nb=nb, b=b: nc.tensor.matmul(kb.ps[jj * 4 + nb][:], lhsT=Gt[:, t, cs * 256 + jj * 128: cs * 256 + (jj + 1) * 128],
                                                                                           rhs=dn[b][:, cs, nb * 512:(nb + 1) * 512], start=(t == 0 and cs == 0), stop=(t == 31 and cs == 1)),
                              reads=[('dn', b)], writes=[('ps', jj * 4 + nb)], signal=(cnt == 16))
        for jj in range(2):
            for nb in range(4):
                e = 'act' if nb % 2 == 0 else 'dve'
                c0 = p * 2048 + nb * 512
                fw.op(e, act_engine_copy(kb, e, yTt[:, jj, c0:c0 + 512], kb.ps[jj * 4 + nb][:]), reads=[('ps', jj * 4 + nb)], writes=[('yTt', jj, p, nb)])
    for jj in range(2):
        cnt = 0
        for tl in range(2):
            for cs in range(2):
                cnt += 1
                fw.op('pe', lambda jj=jj, tl=tl, cs=cs: nc.tensor.matmul(kb.ps[jj][:, 0:NCX], lhsT=Gt[:, 32 + tl, cs * 256 + jj * 128: cs * 256 + (jj + 1) * 128],
                                                                          rhs=dnc[:, tl, cs, :], start=(tl == 0 and cs == 0), stop=(tl == 1 and cs == 1)),
                      reads=['dnc'], writes=[('ps', jj)], signal=(cnt == 4))
        fw.op('act', act_engine_copy(kb, 'act', yTt[:, jj, NL:NT], kb.ps[jj][:, 0:NCX]), reads=[('ps', jj)], writes=[('yTt', jj, 9, 9)])
    kb.fw.barrier()
    for jj in range(2):
        fw.dma('sp', S['yT'][jj * 128:(jj + 1) * 128, :], yTt[:, jj, :], reads=[])


def post_norm_tiles(kb, tag, n, ybanks_of, hres, khres, gbt_of, zt, kz, st, mv, rs, lng, lnb, eps, dst_of, tiles):
    nc, fw = kb.nc, kb.fw
    for i in range(n):
        gb = gbt_of(i)
        for nb in range(2):
            pb = ybanks_of(i, nb)
            fw.op('dve', lambda i=i, nb=nb, pb=pb, gb=gb: nc.vector.tensor_tensor(out=zt[:, i, nb * 512:(nb + 1) * 512], in0=kb.ps[pb][:], in1=gb[:, nb * 512:(nb + 1) * 512], op=ALU.mult),
                  reads=[('ps', pb), 'gbt'], writes=[(kz, i)])
        fw.op('dve', lambda i=i: nc.vector.scalar_tensor_tensor(out=zt[:, i, :], in0=hres[:, i, :], scalar=ALPHA, in1=zt[:, i, :], op0=ALU.mult, op1=ALU.add),
              reads=[(khres, i)], writes=[(kz, i)])
    ln_group(kb, tag + 'pn', zt, kz, n, zt, kz, st, mv, rs, eps)
    for i in range(n):
        fw.op('pool', lambda i=i: nc.gpsimd.tensor_tensor(out=zt[:, i, :], in0=zt[:, i, :], in1=lng[:], op=ALU.mult), reads=['lng'], writes=[(kz, i)])
        fw.op('pool', lambda i=i: nc.gpsimd.tensor_tensor(out=zt[:, i, :], in0=zt[:, i, :], in1=lnb[:], op=ALU.add), reads=['lnb'], writes=[(kz, i)])
        fw.dma('sp', dst_of(tiles[i]), zt[:, i, :], reads=[(kz, i)])


def phase_D1(kb, io, cst, S, l):
    nc, fw = kb.nc, kb.fw
    KC = 8 if l == 0 else 16
    ntile = NTILE if l == 0 else NL // 128
    Wo = kb.sb('Wo', [128, KC, D], BF16)
    wsrc = (io['a_wout'] if l == 0 else io['m_wout']).rearrange("(k p) n -> p k n", p=128)
    for k in range(KC):
        fw.dma('pool', Wo[:, k, :], wsrc[:, k, :], writes=[('Wo', k)])
    gbt = [kb.sb(f'gbt{n}', [128, D], F32) for n in range(2)]
    for n in range(2):
        fw.dma('sp', gbt[n][:], S['gb'][l, 0, n], writes=['gbt'])
    lng = kb.sb('lng', [128, D], F32)
    lnb = kb.sb('lnb', [128, D], F32)
    fw.dma('sp', lng[:], io['ln_g'][l, 0].partition_broadcast(128), writes=['lng'])
    fw.dma('sp', lnb[:], io['ln_b'][l, 0].partition_broadcast(128), writes=['lnb'])
    mT = [kb.sb(f'mT{i}', [128, KC, 128], BF16) for i in range(4)]
    ht = [kb.sb(f'ht{i}', [128, 2, D], F32) for i in range(2)]
    zt = [kb.sb(f'zt{i}', [128, 2, D], F32) for i in range(2)]
    st = kb.sb('st', [128, 2, 2, 6], F32)
    mv = kb.sb('mv', [128, 2, 2], F32)
    rs = kb.sb('rs', [128, 2], F32)
    if l == 0:
        an = [kb.sb(f'an{i}', [128, 768], F32) for i in range(2)]
        sq = kb.sb('sq', [128, 768], F32)
        ss = [kb.sb(f'ss{i}', [128, 6], F32) for i in range(2)]
        hgb = kb.sb('hgb', [128, 128], F32)
        fw.dma('sp', hgb[:], io['da_hg'].partition_broadcast(128), writes=['hgb'])
        fw.op('dve', lambda: nc.vector.tensor_scalar_mul(out=hgb[:], in0=hgb[:], scalar1=1.0 - LAM_INIT), reads=['hgb'], writes=['hgb'])
    hsrc = io['xin'] if l == 0 else S['h2']
    W1p = kb.sb_top('W1p', [128, 8, DFF], BF16)
    W3p = kb.sb_top('W3p', [128, 8, DFF], BF16)
    w1s = io['w_ff1'][l].rearrange("(k p) n -> p k n", p=128)
    w3s = io['w_ff3'][l].rearrange("(k p) n -> p k n", p=128)
    for k in range(8):
        fw.dma('pool', W1p[:, k, :], w1s[:, k, :], writes=[('W1p', k)])
        fw.dma('pool', W3p[:, k, :], w3s[:, k, :], writes=[('W3p', k)])
    S['ffw'] = (W1p, W3p)
    st2 = kb.sb('st2', [128, 2, 2, 6], F32)
    mv2 = kb.sb('mv2', [128, 2, 2], F32)
    rs2 = kb.sb('rs2', [128, 2], F32)
    ng = ntile // 2
    info = {}

    def prologue(g):
        gbuf = g % 2
        for i in range(2):
            t = g * 2 + i
            ti = t
            mb = ti % 4
            ab = ti % 2
            fw.dma('sp', ht[gbuf][:, i, :], hsrc[t * 128:(t + 1) * 128, :], writes=[(('ht', gbuf), i)])
            if l == 0:
                fw.dma('sp', an[ab][:], S['an'][t * 128:(t + 1) * 128, :], writes=[('an', ab)])
                fw.dma('sp', mT[mb][:, 6:8, :], S['yT'][:, t * 128:(t + 1) * 128].rearrange("(k p) n -> p k n", p=128), writes=[('mT', mb, 'f')])
                fw.op('dve', lambda ab=ab: nc.vector.tensor_tensor(out=sq[:], in0=an[ab][:], in1=an[ab][:], op=ALU.mult), reads=[('an', ab)], writes=['sq'])
                fw.op('dve', lambda ab=ab: nc.vector.reduce_sum(out=ss[ab][:], in_=sq[:].rearrange("p (h c) -> p h c", h=6), axis=mybir.AxisListType.X), reads=['sq'], writes=[('ss', ab)])
                fw.op('act', lambda ab=ab: nc.scalar.activation(out=ss[ab][:], in_=ss[ab][:], func=AF.Ln, bias=cst['eps'][:], scale=1.0 / 128), reads=[('ss', ab)], writes=[('ss', ab)])
                fw.op('act', lambda ab=ab: nc.scalar.activation(out=ss[ab][:], in_=ss[ab][:], func=AF.Exp, scale=-0.5), reads=[('ss', ab)], writes=[('ss', ab)])
                for h in range(6):
                    fw.op('dve', lambda ab=ab, h=h: nc.vector.scalar_tensor_tensor(out=an[ab][:, h * 128:(h + 1) * 128], in0=an[ab][:, h * 128:(h + 1) * 128], scalar=ss[ab][:, h:h + 1], in1=hgb[:],
                                                                                   op0=ALU.mult, op1=ALU.mult), reads=[('an', ab), ('ss', ab), 'hgb'], writes=[('an', ab)])
                for grp in range(2):
                    pb = grp
                    for j in range(3):
                        k = grp * 3 + j
                        fw.op('pe', lambda ab=ab, k=k, j=j, pb=pb: nc.tensor.transpose(out=kb.ps[pb][:, j * 128:(j + 1) * 128], in_=an[ab][:, k * 128:(k + 1) * 128], identity=cst['ident'][:]),
                              reads=[('an', ab)], writes=[('ps', pb)])
                    fw.op('act', lambda mb=mb, grp=grp, pb=pb: nc.scalar.copy(out=mT[mb][:, grp * 3:(grp + 1) * 3, :], in_=kb.ps[pb][:, 0:384].rearrange("p (k n) -> p k n", k=3)),
                          reads=[('ps', pb)], writes=[('mT', mb, grp)])
                info[t] = (mb, [('mT', mb, 0), ('mT', mb, 1), ('mT', mb, 'f')])
            else:
                fw.dma('sp', mT[mb][:], S['yT1'][:, t * 128:(t + 1) * 128].rearrange("(k p) n -> p k n", p=128), writes=[('mT', mb, 0)])
                info[t] = (mb, [('mT', mb, 0)])

    def mm(g):
        for i in range(2):
            t = g * 2 + i
            mb, mkeys = info[t]
            for nb in range(2):
                pb = 2 + i * 2 + nb
                for k in range(KC):
                    fw.op('pe', lambda mb=mb, k=k, nb=nb, pb=pb: nc.tensor.matmul(kb.ps[pb][:], lhsT=mT[mb][:, k, :], rhs=Wo[:, k, nb * 512:(nb + 1) * 512], start=(k == 0), stop=(k == KC - 1)),
                          reads=(mkeys + [('Wo', kk) for kk in range(KC)]) if k == 0 else [], writes=[('ps', pb)], signal=(k == KC - 1))

    prologue(0)
    for g in range(ng):
        gbuf = g % 2
        mm(g)
        if g + 1 < ng:
            prologue(g + 1)
        tiles = [g * 2, g * 2 + 1]
        post_norm_tiles(kb, 'D1', 2, lambda i, nb: 2 + i * 2 + nb, ht[gbuf], ('ht', gbuf), lambda i, g=g: gbt[0 if (g * 2 + i) < 32 else 1], zt[gbuf], ('zt', gbuf), st2, mv2, rs2, lng, lnb, cst['eps'],
                        lambda t: S['h1'][t * 128:(t + 1) * 128, :], tiles)


def phase_D2(kb, io, cst, S, l, dst):
    nc, fw = kb.nc, kb.fw
    ntile = NTILE if l == 0 else NL // 128
    NF = DFF // 128
    W1, W3 = S.pop('ffw')
    W2 = kb.sb('W2', [128, NF, D], BF16)
    w2s = io['w_ff2'][l].rearrange("(k p) n -> p k n", p=128)
    for k in range(NF):
        fw.dma('pool', W2[:, k, :], w2s[:, k, :], writes=[('W2', k)])
    wkeys = [('W1', k) for k in range(8)] + [('W3', k) for k in range(8)]
    w2keys = [('W2', k) for k in range(NF)]
    ngb = 2 if l == 0 else 1
    gbt = [kb.sb(f'gbt{n}', [128, D], F32) for n in range(ngb)]
    for n in range(ngb):
        fw.dma('sp', gbt[n][:], S['gb'][l, 1, n], writes=['gbt'])
    lng = kb.sb('lng', [128, D], F32)
    lnb = kb.sb('lnb', [128, D], F32)
    fw.dma('sp', lng[:], io['ln_g'][l, 1].partition_broadcast(128), writes=['lng'])
    fw.dma('sp', lnb[:], io['ln_b'][l, 1].partition_broadcast(128), writes=['lnb'])
    ht = [kb.sb(f'ht{i}', [128, 2, D], F32) for i in range(2)]
    xh = kb.sb('xh', [128, 2, D], F32)
    uT = kb.sb('uT', [128, 8, 256], BF16)
    gT = kb.sb('gT', [128, NF, 256], BF16)
    sil = [kb.sb(f'sil{i}', [128, 256], F32) for i in range(2)]
    zt = kb.sb('zt', [128, 2, D], F32)
    st = kb.sb('st', [128, 2, 2, 6], F32)
    mv = kb.sb('mv', [128, 2, 2], F32)
    rs = kb.sb('rs', [128, 2], F32)
    st2 = kb.sb('st2', [128, 2, 2, 6], F32)
    mv2 = kb.sb('mv2', [128, 2, 2], F32)
    rs2 = kb.sb('rs2', [128, 2], F32)
    ng = ntile // 2

    def prologue(g):
        gbuf = g % 2
        tok0 = g * 256
        for i in range(2):
            fw.dma('sp', ht[gbuf][:, i, :], S['h1'][tok0 + i * 128: tok0 + (i + 1) * 128, :], writes=[(('ht', gbuf), i)])
        ln_group(kb, 'F', ht[gbuf], ('ht', gbuf), 2, xh, 'xh', st, mv, rs, cst['eps'])
        transpose_mod(kb, 'F', xh, 'xh', 2, tok0, uT, 'uT', cst, l, 3, 4, (0, 1))

    sic = [0]

    def up(g):
        uk = [('uT', i, k) for i in range(2) for k in range(8)]
        for fc in range(NF):
            pb = 2 + fc % 2
            for wi, Wm in enumerate((W1, W3)):
                for k in range(8):
                    fw.op('pe', lambda Wm=Wm, wi=wi, k=k, fc=fc, pb=pb: nc.tensor.matmul(kb.ps[pb][:, wi * 256:(wi + 1) * 256], lhsT=Wm[:, k, fc * 128:(fc + 1) * 128], rhs=uT[:, k, :],
                                                                                          start=(k == 0), stop=(k == 7)),
                          reads=(uk + wkeys) if (k == 0 and wi == 0) else [], writes=[('ps', pb)], signal=(k == 7 and wi == 1))
            sb_ = sic[0] % 2
            sic[0] += 1
            fw.op('act', lambda pb=pb, sb_=sb_: nc.scalar.activation(out=sil[sb_][:], in_=kb.ps[pb][:, 0:256], func=AF.Silu), reads=[('ps', pb)], writes=[('sil', sb_)])
            fw.op('dve', lambda pb=pb, sb_=sb_, fc=fc: nc.vector.tensor_tensor(out=gT[:, fc, :], in0=sil[sb_][:], in1=kb.ps[pb][:, 256:512], op=ALU.mult),
                  reads=[('ps', pb), ('sil', sb_)], writes=[('gT', fc)])

    def down(g):
        gk = [('gT', fc) for fc in range(NF)]
        for i in range(2):
            for nb in range(2):
                pb = 4 + i * 2 + nb
                for fc in range(NF):
                    fw.op('pe', lambda i=i, nb=nb, fc=fc, pb=pb: nc.tensor.matmul(kb.ps[pb][:], lhsT=gT[:, fc, i * 128:(i + 1) * 128], rhs=W2[:, fc, nb * 512:(nb + 1) * 512],
                                                                                  start=(fc == 0), stop=(fc == NF - 1)),
                          reads=(gk + w2keys) if fc == 0 else [], writes=[('ps', pb)], signal=(fc == NF - 1))

    kb.top_release = True
    prologue(0)
    for g in range(ng):
        gbuf = g % 2
        up(g)
        if g + 1 < ng:
            prologue(g + 1)
        down(g)
        tiles = [g * 2, g * 2 + 1]
        post_norm_tiles(kb, 'D2', 2, lambda i, nb: 4 + i * 2 + nb, ht[gbuf], ('ht', gbuf), lambda i, g=g: gbt[0 if (g * 2 + i) < 32 else 1], zt, 'zt', st2, mv2, rs2, lng, lnb, cst['eps'],
                        lambda t: dst[t * 128:(t + 1) * 128, :], tiles)


def phase_M1(kb, io, cst, S):
    nc, fw = kb.nc, kb.fw
    W = kb.sb('Wm', [128, 8, 4096], BF16)
    wv = io['m_win'].rearrange("(k p) n -> p k n", p=128)
    for k in range(8):
        fw.dma('pool', W[:, k, :], wv[:, k, :], writes=[('Wm', k)])
    Wk = [('Wm', k) for k in range(8)]
    xt = [kb.sb(f'xt{i}', [128, 4, 1024], F32) for i in range(2)]
    uT = [kb.sb(f'uT{i}', [128, 8, 512], BF16) for i in range(2)]
    st = kb.sb('st', [128, 4, 2, 6], F32)
    mv = kb.sb('mv', [128, 4, 2], F32)
    rstd = kb.sb('rstd', [128, 4], F32)
    ob = [kb.sb(f'ob{i}', [128, 4, 512], BF16) for i in range(3)]
    ngroups = (NT + 511) // 512
    obi = 0
    pbi = 0
    for g in range(ngroups):
        tok0 = g * 512
        ntok = min(512, NT - tok0)
        n = ntok // 128
        b = g % 2
        for i in range(n):
            fw.dma('sp', xt[b][:, i, :], S['h2'][tok0 + i * 128: tok0 + (i + 1) * 128, :], writes=[(('xt', b), i)])
        ln_group(kb, 'M', xt[b], ('xt', b), n, xt[b], ('xt', b), st, mv, rstd, cst['eps'])
        transpose_mod(kb, 'M', xt[b], ('xt', b), n, tok0, uT[b], ('uT', b), cst, 1, 0, 1, (0, 1))
        uk = [(('uT', b), i, k) for i in range(n) for k in range(8)]
        ncg = 8 if tok0 < NL else 4
        for cg in range(ncg):
            o = obi % 3
            obi += 1
            for cc in range(4):
                c = cg * 4 + cc
                pa = 2 + pbi % 6
                pbi += 1
                for k in range(8):
                    fw.op('pe', lambda k=k, pa=pa, c=c, b=b, ntok=ntok: nc.tensor.matmul(kb.ps[pa][:, 0:ntok], lhsT=W[:, k, c * 128:(c + 1) * 128], rhs=uT[b][:, k, 0:ntok], start=(k == 0), stop=(k == 7)),
                          reads=uk + Wk if k == 0 else [], writes=[('ps', pa)], signal=(k == 7))
                if c < 16:
                    e = 'dve' if cc % 2 == 0 else 'act'
                    fw.op(e, act_engine_copy(kb, e, ob[o][:, cc, 0:ntok], kb.ps[pa][:, 0:ntok]), reads=[('ps', pa)], writes=[('ob', o, cc)])
                else:
                    fw.op('act', lambda pa=pa, o=o, cc=cc, ntok=ntok: nc.scalar.activation(out=ob[o][:, cc, 0:ntok], in_=kb.ps[pa][:, 0:ntok], func=AF.Silu), reads=[('ps', pa)], writes=[('ob', o, cc)])
            if cg < 4:
                dst = S['xmT'][cg * 512:(cg + 1) * 512, tok0:tok0 + ntok]
            else:
                dst = S['szT'][(cg - 4) * 512:(cg - 3) * 512, tok0:tok0 + ntok]
            fw.dma('sp', dst.rearrange("(c p) n -> p c n", p=128), ob[o][:, :, 0:ntok], reads=[('ob', o, cc) for cc in range(4)])


def phase_M2(kb, io, cst, S):
    nc, fw = kb.nc, kb.fw
    cw = kb.sb('cw', [128, 16, 5], F32)
    cb = kb.sb('cb', [128, 16], F32)
    fw.dma('sp', cw[:], io['convw'], writes=['cw'])
    fw.dma('sp', cb[:], io['convb'], writes=['cb'])
    dgw = kb.sb('dgw', [128, 16, 5, 128], BF16)
    for c in range(16):
        for j in range(5):
            fw.op('dve', lambda c=c, j=j: nc.vector.tensor_scalar_mul(out=dgw[:, c, j, :], in0=cst['ident'][:], scalar1=cw[:, c, j:j + 1]), reads=['cw'], writes=[('dgw', c, j)])
    bd = {}
    for nm in ('bdq', 'bdk', 'bdv'):
        bd[nm] = kb.sb(nm, [128, 16, 128], BF16)
        fw.dma('pool', bd[nm][:], io[nm].rearrange("c p n -> p c n"), writes=[nm])
    Wg = kb.sb('Wg', [128, 48, 32], BF16)
    fw.dma('pool', Wg[:], io['wg'].rearrange("(c p) n -> p c n", p=128), writes=['Wg'])
    BS = 256
    NB = BS // 128
    xm = [kb.sb(f'xm{i}', [128, 16, BS + 4], BF16) for i in range(2)]
    xc = [kb.sb(f'xc{i}', [128, 16, BS], BF16) for i in range(2)]
    qf = [kb.sb(f'qf{i}', [128, 16, BS], BF16) for i in range(2)]
    kf = [kb.sb(f'kf{i}', [128, 16, BS], BF16) for i in range(2)]
    vf = [kb.sb(f'vf{i}', [128, 16, BS], BF16) for i in range(2)]
    ktk = [kb.sb(f'ktk{i}', [128, NB, 2048], BF16) for i in range(2)]
    vtk = [kb.sb(f'vtk{i}', [128, NB, 2048], BF16) for i in range(2)]
    gsb = [kb.sb(f'gsb{i}', [32, BS], F32) for i in range(2)]
    kb.fw.barrier()
    pbc = [0]

    def nextbank():
        pa = pbc[0] % 7
        pbc[0] += 1
        return pa

    nblk = NT // BS
    nlat = NL // BS
    for blk in range(nblk):
        tok0 = blk * BS
        ntok = BS
        b = blk % 2
        lat = blk < nlat
        fw.op('pool', lambda b=b: nc.gpsimd.memset(xm[b][:], 0.0), writes=[('xm', b)])
        lo = tok0 - 2 if (lat and blk > 0) else tok0
        hi = tok0 + ntok + 2 if (lat and blk < nlat - 1) else tok0 + ntok
        fw.dma('sp', xm[b][:, :, 2 - (tok0 - lo): 2 + (hi - tok0)], S['xmT'][:, lo:hi].rearrange("(c p) n -> p c n", p=128), writes=[('xm', b)])

        def conv(c, b=b):
            pa = nextbank()
            for j in range(5):
                fw.op('pe', lambda c=c, j=j, pa=pa: nc.tensor.matmul(kb.ps[pa][:, 0:ntok], lhsT=dgw[:, c, j, :], rhs=xm[b][:, c, j:j + ntok], start=(j == 0), stop=(j == 4)),
                      reads=[('xm', b)] + [('dgw', c, jj) for jj in range(5)] if j == 0 else [], writes=[('ps', pa)], signal=(j == 4))
            fw.op('act', lambda c=c, pa=pa: nc.scalar.activation(out=xc[b][:, c, :], in_=kb.ps[pa][:, 0:ntok], func=AF.Silu, bias=cb[:, c:c + 1], scale=1.0),
                  reads=[('ps', pa), 'cb'], writes=[('xc', b, c)])

        def qkv(c, b=b):
            for nm, src, dstt, kk in (('bdq', 'xc', qf[b], 'qf'), ('bdk', 'xc', kf[b], 'kf'), ('bdv', 'xm', vf[b], 'vf')):
                pa = nextbank()
                rhs = xc[b][:, c, :] if src == 'xc' else xm[b][:, c, 2:2 + ntok]
                fw.op('pe', lambda nm=nm, c=c, pa=pa, rhs=rhs: nc.tensor.matmul(kb.ps[pa][:, 0:ntok], lhsT=bd[nm][:, c, :], rhs=rhs, start=True, stop=True),
                      reads=[nm, ('xc', b, c) if src == 'xc' else ('xm', b)], writes=[('ps', pa)])
                e = 'dve' if kk != 'kf' else 'act'
                fw.op(e, act_engine_copy(kb, e, dstt[:, c, :], kb.ps[pa][:, 0:ntok]), reads=[('ps', pa)], writes=[(kk, b, c)])

        def gates(c, b=b):
            for gi, (src, kk) in enumerate(((qf[b], 'qf'), (kf[b], 'kf'), (vf[b], 'vf'))):
                fw.op('pe', lambda gi=gi, c=c, src=src: nc.tensor.matmul(kb.ps[7][0:32, 0:ntok], lhsT=Wg[:, gi * 16 + c, :], rhs=src[:, c, :],
                                                                        start=(c == 0 and gi == 0), stop=(c == 15 and gi == 2)),
                      reads=[(kk, b, c), 'Wg'], writes=[('ps', 7)], signal=(c == 15 and gi == 2))

        conv(0)
        qkv(0)
        for c in range(16):
            if c + 1 < 16:
                conv(c + 1)
            gates(c)
            if c + 1 < 16:
                qkv(c + 1)
        fw.op('dve', lambda b=b: nc.vector.tensor_copy(out=gsb[b][:], in_=kb.ps[7][0:32, 0:ntok]), reads=[('ps', 7)], writes=[('gsb', b)])
        fw.dma('sp', S['gT'][:, tok0:tok0 + ntok], gsb[b][:], reads=[('gsb', b)])
        for which, srct, bdn, dstt, kk in ((0, xc[b], 'bdk', ktk[b], 'ktk'), (1, xm[b], 'bdv', vtk[b], 'vtk')):
            for i in range(NB):
                for cg in range(4):
                    pa = nextbank()
                    for cc in range(4):
                        c = cg * 4 + cc
                        off = 0 if which == 0 else 2
                        fw.op('pe', lambda srct=srct, c=c, cc=cc, i=i, pa=pa, bdn=bdn, off=off: nc.tensor.matmul(
                            kb.ps[pa][:, cc * 128:(cc + 1) * 128], lhsT=srct[:, c, off + i * 128: off + (i + 1) * 128], rhs=bd[bdn][:, c, :], start=True, stop=True),
                            reads=[('xc', b, c) if which == 0 else ('xm', b), bdn], writes=[('ps', pa)], signal=(cc == 3))
                    e = 'act' if (cg % 2 == 0) else 'dve'
                    fw.op(e, act_engine_copy(kb, e, dstt[:, i, cg * 512:(cg + 1) * 512], kb.ps[pa][:]), reads=[('ps', pa)], writes=[(kk, b, i, cg)])
                dst = S['ktok'] if which == 0 else S['vtok']
                fw.dma('sp' if which == 0 else 'act', dst[tok0 + i * 128: tok0 + (i + 1) * 128, :], dstt[:, i, :], reads=[(kk, b, i, cg) for cg in range(4)])
        allc = lambda kk: [(kk, b, c) for c in range(16)]
        fw.dma('sp', S['qT1'][:, tok0:tok0 + ntok].rearrange("(c p) n -> p c n", p=128), qf[b][:], reads=allc('qf'))
        fw.dma('act', S['kT1'][:, tok0:tok0 + ntok].rearrange("(c p) n -> p c n", p=128), kf[b][:], reads=allc('kf'))
        if lat:
            fw.dma('sp', S['xcT'][:, tok0:tok0 + ntok].rearrange("(c p) n -> p c n", p=128), xc[b][:], reads=allc('xc'))


def phase_M3(kb, io, cst, S, T):
    nc, fw = kb.nc, kb.fw
    gsb = kb.sb('gsb', [32, NT], F32)
    bgc = kb.sb('bgc', [32, 1], F32)
    G = kb.sb('G', [128, NTILE, 32], F32)
    NLF = kb.sb('NLF', [128, NTILE, 16], F32)
    tri = kb.sb('tri', [128, 2, 128], F32)
    one = kb.sb('one', [128, 1], F32)
    A = kb.sb('A', [128, NTILE, 16], F32)
    fw.dma('sp', gsb[:], S['gT'], writes=['gsb'])
    fw.dma('sp', bgc[:], io['bg'].rearrange("(p o) -> p o", o=1), writes=['bgc'])
    fw.dma('sp', tri[:], io['tri'].rearrange("d p n -> p d n"), writes=['tri'])
    fw.op('pool', lambda: nc.gpsimd.memset(one[:], 1.0), writes=['one'])
    fw.op('dve', lambda: nc.vector.tensor_scalar_add(out=gsb[:], in0=gsb[:], scalar1=bgc[:]), reads=['gsb', 'bgc'], writes=['gsb'])
    if M3_STEPS < 2:
        return
    for t in range(NTILE):
        pb = t // 16
        fw.op('pe', lambda t=t, pb=pb: nc.tensor.matmul(kb.ps[pb][:, (t % 16) * 32:(t % 16 + 1) * 32], lhsT=gsb[0:32, t * 128:(t + 1) * 128], rhs=cst['ident'][0:32, 0:32], start=True, stop=True),
              reads=['gsb'], writes=[('ps', pb)])
    for pb, (t0, t1) in enumerate(((0, 16), (16, 32), (32, 34))):
        fw.op('dve', lambda pb=pb, t0=t0, t1=t1: nc.vector.tensor_copy(out=G[:, t0:t1, :], in_=kb.ps[pb][:, 0:(t1 - t0) * 32].rearrange("p (t c) -> p t c", c=32)),
              reads=[('ps', pb)], writes=[('G', pb)])
    gk = [('G', i) for i in range(3)]
    if M3_STEPS < 3:
        return
    fw.op('act', lambda: nc.scalar.activation(out=NLF[:], in_=G[:, :, 16:32], func=AF.Exp, scale=-1.0), reads=gk, writes=['NLF'])
    fw.op('act', lambda: nc.scalar.activation(out=NLF[:], in_=NLF[:], func=AF.Ln, bias=one[:], scale=1.0), reads=['NLF', 'one'], writes=['NLF'])
    if M3_STEPS < 4:
        return
    NLd = kb.sb('NLd', [128, 2, NTILE, 8], F32)
    NBs = kb.sb('NBs', [128, 2, NTILE, 8], F32)
    NEs = kb.sb('NEs', [128, 2, NTILE, 8], F32)
    for d in range(2):
        fw.op('dve', lambda d=d: nc.vector.tensor_copy(out=NLd[:, d, :, :], in_=NLF[:, :, d * 8:(d + 1) * 8]), reads=['NLF'], writes=[('NLd', d)])
        fw.op('pe', lambda d=d: nc.tensor.matmul(kb.ps[4 + d][:, 0:NTILE * 8], lhsT=tri[:, d, :], rhs=NLd[:, d, :, :].rearrange("p t c -> p (t c)"), start=True, stop=True),
              reads=[('NLd', d), 'tri'], writes=[('ps', 4 + d)])
        fw.op('pe', lambda d=d: nc.tensor.matmul(kb.ps[6 + d][:, 0:NTILE * 8], lhsT=cst['ones'][:], rhs=NLd[:, d, :, :].rearrange("p t c -> p (t c)"), start=True, stop=True),
              reads=[('NLd', d)], writes=[('ps', 6 + d)])
        fw.op('dve', lambda d=d: nc.vector.tensor_copy(out=NBs[:, d, :, :].rearrange("p t c -> p (t c)"), in_=kb.ps[4 + d][:, 0:NTILE * 8]), reads=[('ps', 4 + d)], writes=[('NBs', d)])
        fw.op('dve', lambda d=d: nc.vector.tensor_copy(out=NEs[:, d, :, :].rearrange("p t c -> p (t c)"), in_=kb.ps[6 + d][:, 0:NTILE * 8]), reads=[('ps', 6 + d)], writes=[('NEs', d)])
    if M3_STEPS < 5:
        return
    for d in range(2):
        nb = NBs[:, d, :, :]
        ne = NEs[:, d, :, :]
        sl = slice(d * 8, (d + 1) * 8)
        fw.op('act', lambda nb=nb, sl=sl: nc.scalar.activation(out=T['EQ'][:, :, sl], in_=nb, func=AF.Exp, scale=-1.0), reads=[('NBs', d)], writes=[('EQ', d)])
        fw.op('dve', lambda nb=nb, sl=sl: nc.vector.tensor_tensor(out=A[:, :, sl], in0=nb, in1=G[:, :, sl], op=ALU.add), reads=[('NBs', d)] + gk, writes=[('A', d)])
        fw.op('act', lambda sl=sl: nc.scalar.activation(out=T['EKS'][:, :, sl], in_=A[:, :, sl], func=AF.Exp), reads=[('A', d)], writes=[('EKS', d)])
        fw.op('dve', lambda sl=sl: nc.vector.tensor_scalar_mul(out=T['EKS'][:, :, sl], in0=T['EKS'][:, :, sl], scalar1=1.0 / 16), reads=[('EKS', d)], writes=[('EKS', d)])
        fw.op('act', lambda ne=ne, sl=sl: nc.scalar.activation(out=T['E'][:, :, sl], in_=ne, func=AF.Exp, scale=-1.0), reads=[('NEs', d)], writes=[('E', d)])
        fw.op('dve', lambda sl=sl: nc.vector.tensor_tensor(out=T['EKW'][:, :, sl], in0=T['EKS'][:, :, sl], in1=T['E'][:, :, sl], op=ALU.mult), reads=[('EKS', d), ('E', d)], writes=[('EKW', d)])


def phase_M4(kb, io, cst, S, T):
    nc, fw = kb.nc, kb.fw
    tri = kb.sb('tri', [128, 2, 128], F32)
    skp = kb.sb('skp', [128, 16], F32)
    mhg = kb.sb('mhg', [128, 16], F32)
    fw.dma('sp', tri[:], io['tri'].rearrange("d p n -> p d n"), writes=['tri'])
    fw.dma('sp', skp[:], io['skipT'], writes=['skp'])
    fw.dma('sp', mhg[:], io['mhgT'], writes=['mhg'])
    qTh = kb.sb('qTh', [128, 2, NT], BF16)
    kTh = kb.sb('kTh', [128, 2, NT], BF16)
    ktk = kb.sb('ktk', [128, NTILE, 256], BF16)
    vext = kb.sb('vext', [128, NTILE, 257], BF16)
    xcTh = kb.sb('xcTh', [128, 2, NL], BF16)
    szTh = kb.sb('szTh', [128, 2, NL], BF16)
    hbuf = kb.sb('hbuf', [128, 32, 256], F32)
    yTh = kb.sb('yTh', [128, 2, NL], BF16)
    Cf = [kb.sb(f'Cf{d}', [128, 2, 257], F32) for d in range(2)]
    Cb = [kb.sb(f'Cb{d}', [128, 2, 257], BF16) for d in range(2)]
    sm = [kb.sb(f'sm{d}', [128, 128], BF16) for d in range(2)]
    vw2 = [[kb.sb(f'vw{d}{i}', [128, 257], BF16) for i in range(2)] for d in range(2)]
    nd = [kb.sb(f'nd{d}', [128, 257], F32) for d in range(2)]
    dn = [kb.sb(f'dn{d}', [128, 2], F32) for d in range(2)]
    hs = kb.sb('hs', [128, 256], F32)
    st = kb.sb('st', [128, 6], F32)
    mv = kb.sb('mv', [128, 2], F32)
    rs = kb.sb('rs', [128, 1], F32)
    tmp = [kb.sb(f'tmp{j}', [128, 128], F32) for j in range(4)]
    st4 = kb.sb('st4', [128, 4, 6], F32)
    mv4 = [kb.sb(f'mv4{i}', [128, 4, 2], F32) for i in range(2)]
    rs4 = [kb.sb(f'rs4{i}', [128, 4], F32) for i in range(2)]
    fw.op('pool', lambda: nc.gpsimd.memset(vext[:], 1.0), writes=['vext'])
    order = [[32, 33] + list(range(32)), [33, 32] + list(range(31, -1, -1))]
    for h in range(8):
        fw.dma('sp', qTh[:], S['qT1'][h * 256:(h + 1) * 256, :].rearrange("(j p) n -> p j n", p=128), writes=['qTh'])
        fw.dma('sp', kTh[:], S['kT1'][h * 256:(h + 1) * 256, :].rearrange("(j p) n -> p j n", p=128), writes=['kTh'])
        fw.dma('sp', ktk[:], S['ktok'][:, h * 256:(h + 1) * 256].rearrange("(t p) c -> p t c", p=128), writes=['ktk'])
        fw.dma('pool', vext[:, :, 0:256], S['vtok'][:, h * 256:(h + 1) * 256].rearrange("(t p) c -> p t c", p=128), writes=['vext'])
        fw.dma('sp', xcTh[:], S['xcT'][h * 256:(h + 1) * 256, :].rearrange("(j p) n -> p j n", p=128), writes=['xcTh'])
        fw.dma('sp', szTh[:], S['szT'][h * 256:(h + 1) * 256, :].rearrange("(j p) n -> p j n", p=128), writes=['szTh'])
        for j in range(2):
            fw.op('dve', lambda j=j, h=h: nc.vector.tensor_scalar_mul(out=xcTh[:, j, :], in0=xcTh[:, j, :], scalar1=skp[:, 2 * h + j:2 * h + j + 1]), reads=['xcTh', 'skp'], writes=['xcTh'])
        for d in range(2):
            for j in range(2):
                fw.op('pool', lambda d=d, j=j: nc.gpsimd.memset(Cf[d][:, j, :], 0.0), writes=[('Cf', d, j)])
            fw.op('pool', lambda d=d: nc.gpsimd.memset(Cb[d][:], 0.0), writes=[('Cb', d)])
        def stV(step, h=h):
            if step >= NTILE - 1:
                return
            for d in range(2):
                c = order[d][step]
                col = d * 8 + h
                vb = vw2[d][step % 2]
                fw.op('act', lambda d=d, c=c, col=col, vb=vb: nc.scalar.activation(out=vb[:], in_=vext[:, c, :], func=AF.Identity, scale=T['EKW'][:, c, col:col + 1]), reads=['vext'], writes=[('vw', d, step % 2)])

        def stS(step, h=h):
            for d in range(2):
                c = order[d][step]
                col = d * 8 + h
                cs = slice(c * 128, (c + 1) * 128)
                if c < 32:
                    for j in range(2):
                        fw.op('pe', lambda j=j, d=d, cs=cs: nc.tensor.matmul(kb.ps[d][:, 0:128], lhsT=kTh[:, j, cs], rhs=qTh[:, j, cs], start=(j == 0), stop=(j == 1)),
                              reads=['qTh', 'kTh'], writes=[('ps', d)], signal=(j == 1))
                    fw.op('dve', lambda d=d, c=c, col=col: nc.vector.scalar_tensor_tensor(out=sm[d][:], in0=kb.ps[d][:, 0:128], scalar=T['EKS'][:, c, col:col + 1], in1=tri[:, d, :],
                                                                                          op0=ALU.mult, op1=ALU.mult), reads=[('ps', d), 'tri'], writes=[('sm', d)])

        def stD(step, h=h):
            if step >= NTILE - 1:
                return
            for d in range(2):
                c = order[d][step]
                vb = vw2[d][step % 2]
                for j in range(2):
                    bk = 4 + 2 * j + d
                    fw.op('pe', lambda j=j, bk=bk, c=c, vb=vb: nc.tensor.matmul(kb.ps[bk][:, 0:257], lhsT=ktk[:, c, j * 128:(j + 1) * 128], rhs=vb[:], start=True, stop=True),
                          reads=[('vw', d, step % 2), 'ktk'], writes=[('ps', bk)])

        def stB(step, h=h):
            for d in range(2):
                c = order[d][step]
                col = d * 8 + h
                cs = slice(c * 128, (c + 1) * 128)
                if c < 32:
                    nbk = 2 + d
                    for j in range(2):
                        fw.op('pe', lambda j=j, nbk=nbk, cs=cs, d=d: nc.tensor.matmul(kb.ps[nbk][:, 0:257], lhsT=qTh[:, j, cs], rhs=Cb[d][:, j, :], start=(j == 0), stop=False),
                              reads=[('Cb', d)], writes=[('ps', nbk)], signal=False)
                    fw.op('pe', lambda nbk=nbk, c=c, d=d: nc.tensor.matmul(kb.ps[nbk][:, 0:257], lhsT=sm[d][:], rhs=vext[:, c, :], start=False, stop=True),
                          reads=[('sm', d), 'vext'], writes=[('ps', nbk)])
                    fw.op('act', lambda nbk=nbk, d=d, c=c, col=col: nc.scalar.activation(out=nd[d][:], in_=kb.ps[nbk][:, 0:257], func=AF.Identity, scale=T['EQ'][:, c, col:col + 1]),
                          reads=[('ps', nbk)], writes=[('nd', d)])

        def stU(step, h=h):
            if step >= NTILE - 1:
                return
            for d in range(2):
                c = order[d][step]
                col = d * 8 + h
                fw.op('dve', lambda d=d, c=c, col=col: nc.vector.scalar_tensor_tensor(out=Cf[d][:], in0=Cf[d][:], scalar=T['E'][:, c, col:col + 1], in1=kb.psall[:, 4 + d:8:2, 0:257],
                                                                                      op0=ALU.mult, op1=ALU.add), reads=[('ps', 4 + d), ('ps', 6 + d)], writes=[('Cf', d, 0), ('Cf', d, 1)])
                fw.op('act', lambda d=d: nc.scalar.copy(out=Cb[d][:], in_=Cf[d][:]), reads=[('Cf', d, 0), ('Cf', d, 1)], writes=[('Cb', d)])

        def stO(step, h=h):
            for d in range(2):
                c = order[d][step]
                if c >= 32:
                    continue
                cs = slice(c * 128, (c + 1) * 128)
                sbk = d
                fw.op('dve', lambda d=d: nc.vector.scalar_tensor_tensor(out=dn[d][:, 0:1], in0=nd[d][:, 256:257], scalar=-1.0, in1=nd[d][:, 256:257], op0=ALU.mult, op1=ALU.max), reads=[('nd', d)], writes=[('dn', d)])
                fw.op('dve', lambda d=d: nc.vector.tensor_scalar_max(out=dn[d][:, 0:1], in0=dn[d][:, 0:1], scalar1=1.0), reads=[('dn', d)], writes=[('dn', d)])
                fw.op('dve', lambda d=d: nc.vector.reciprocal(out=dn[d][:, 1:2], in_=dn[d][:, 0:1]), reads=[('dn', d)], writes=[('dn', d)])
                first = (c < 16) if d == 0 else (c >= 16)
                if first:
                    fw.op('dve', lambda d=d, c=c: nc.vector.tensor_scalar_mul(out=hbuf[:, c, :], in0=nd[d][:, 0:256], scalar1=dn[d][:, 1:2]), reads=[('nd', d), ('dn', d)], writes=[('hbuf', c)])
                    continue
                fw.op('dve', lambda d=d, c=c: nc.vector.scalar_tensor_tensor(out=hbuf[:, c, :], in0=nd[d][:, 0:256], scalar=dn[d][:, 1:2], in1=hbuf[:, c, :], op0=ALU.mult, op1=ALU.add),
                      reads=[('nd', d), ('dn', d)], writes=[('hbuf', c)])

        def fin_stats(g):
            for i in range(4):
                c = g * 4 + i
                fw.op('dve', lambda c=c, i=i: nc.vector.bn_stats(out=st4[:, i, :], in_=hbuf[:, c, :]), reads=[('hbuf', c)], writes=[('st4', i)])
                fw.op('dve', lambda i=i: nc.vector.bn_aggr(out=mv4[g % 2][:, i, :], in_=st4[:, i, :]), reads=[('st4', i)], writes=[('mv4', g % 2, i)])
            fw.op('act', lambda: nc.scalar.activation(out=rs4[g % 2][:], in_=mv4[g % 2][:, :, 1], func=AF.Ln, bias=cst['eps'][:], scale=1.0), reads=[('mv4', g % 2, i) for i in range(4)], writes=[('rs4', g % 2)])
            fw.op('act', lambda: nc.scalar.activation(out=rs4[g % 2][:], in_=rs4[g % 2][:], func=AF.Exp, scale=-0.5), reads=[('rs4', g % 2)], writes=[('rs4', g % 2)])

        def fin_rest(g, h=h):
            for i in range(4):
                c = g * 4 + i
                cs = slice(c * 128, (c + 1) * 128)
                fw.op('dve', lambda c=c, i=i: nc.vector.tensor_scalar(out=hbuf[:, c, :], in0=hbuf[:, c, :], scalar1=mv4[g % 2][:, i, 0:1], scalar2=rs4[g % 2][:, i:i + 1], op0=ALU.subtract, op1=ALU.mult),
                      reads=[('mv4', g % 2, i), ('rs4', g % 2)], writes=[('hbuf', c)])
                pbk = (g * 4 + i) % 4
                for j in range(2):
                    fw.op('pe', lambda j=j, c=c, pbk=pbk: nc.tensor.transpose(out=kb.ps[pbk][:, j * 128:(j + 1) * 128], in_=hbuf[:, c, j * 128:(j + 1) * 128], identity=cst['ident'][:]),
                          reads=[('hbuf', c)], writes=[('ps', pbk)])
                for j in range(2):
                    tb = (i * 2 + j) % 4
                    fw.op('dve', lambda j=j, pbk=pbk, cs=cs, tb=tb: nc.vector.scalar_tensor_tensor(out=tmp[tb][:], in0=kb.ps[pbk][:, j * 128:(j + 1) * 128], scalar=mhg[:, 2 * h + j:2 * h + j + 1],
                                                                                                   in1=xcTh[:, j, cs], op0=ALU.mult, op1=ALU.add),
                          reads=[('ps', pbk), 'xcTh', 'mhg'], writes=[('tmp', tb)])
                    fw.op('pool', lambda j=j, cs=cs, tb=tb: nc.gpsimd.tensor_tensor(out=yTh[:, j, cs], in0=tmp[tb][:], in1=szTh[:, j, cs], op=ALU.mult), reads=[('tmp', tb), 'szTh'], writes=['yTh'])

        stV(0)
        stS(0)
        stD(0)
        for step in range(NTILE):
            if step + 1 < NTILE:
                stV(step + 1)
            stB(step)
            stU(step)
            if step + 1 < NTILE:
                stS(step + 1)
                stD(step + 1)
            stO(step)
        fin_stats(0)
        for g in range(8):
            if g + 1 < 8:
                fin_stats(g + 1)
            fin_rest(g)
        fw.dma('sp', S['yT1'][h * 256:(h + 1) * 256, :].rearrange("(j p) n -> p j n", p=128), yTh[:], reads=['yTh'])


_NC_CACHE = {}


def kernel(**inputs):
    inp = {k: np.asarray(v) for k, v in inputs.items()}
    sh = prep_shared(inp)
    if 'nc' not in _NC_CACHE:
        _NC_CACHE['nc'] = build()
    nc = _NC_CACHE['nc']
    in_maps = []
    for b in range(8):
        m = dict(sh)
        m.update(prep_core(inp, b))
        in_maps.append(m)
    res = run_bass_kernel_spmd(nc, in_maps, core_ids=list(range(8)))
    out = np.stack([np.asarray(r['out'], dtype=np.float32) for r in res.results], 0)
    return out
```

```python
import math
import numpy as np
import ml_dtypes
import concourse.bass as bass
import concourse.mybir as mybir
from concourse.bass_utils import run_bass_kernel_spmd

F32 = mybir.dt.float32
BF16 = mybir.dt.bfloat16
U8 = mybir.dt.uint8
ALU = mybir.AluOpType
AF = mybir.ActivationFunctionType

D = 1024
NL = 4096
NCX = 256
NT = NL + NCX
NTILE = NT // 128
DFF = 2816
ALPHA = 4 ** 0.25
LN_EPS = 1e-5
LAM_INIT = 0.8 - 0.6 * math.exp(0.0)
M3_STEPS = 9


class FW:
    NS = 8

    def __init__(self, nc):
        self.nc = nc
        self.engs = ('pe', 'dve', 'act', 'pool', 'sp')
        self.csem = {e: nc.alloc_semaphore("c_" + e) for e in ('pe', 'dve', 'act', 'pool')}
        self.ccnt = {e: 0 for e in self.csem}
        self.pending = {e: False for e in self.csem}
        self.dq = {q: dict(sems=[nc.alloc_semaphore(f"d_{q}{i}") for i in range(self.NS)], n=0)
                   for q in ('sp', 'pool', 'act')}
        self.known = {e: {} for e in self.engs}
        self.buf = {}
        self.prog = {e: [] for e in self.engs}
        self.ninstr = 0

    def _deps(self, reads, writes):
        ev = {}

        def add(e):
            if e is None:
                return
            s, v = e
            if s.num not in ev or ev[s.num][1] < v:
                ev[s.num] = (s, v)
        for r in reads:
            b = self.buf.get(r)
            if b:
                add(b['w'])
        for w in writes:
            b = self.buf.get(w)
            if b:
                add(b['w'])
                for e in b['r'].values():
                    add(e)
        return ev

    def _wait(self, e, ev, skip_sem=None):
        kn = self.known[e]
        for k, (s, v) in ev.items():
            if skip_sem is not None and s is skip_sem:
                continue
            if kn.get(k, 0) < v:
                self.prog[e].append(('w', s, v))
                kn[k] = v

    def _record(self, reads, writes, event):
        for r in reads:
            b = self.buf.setdefault(r, dict(w=None, r={}))
            b['r'][event[0].num] = event
        for w in writes:
            self.buf[w] = dict(w=event, r={})

    def op(self, e, fn, reads=(), writes=(), signal=True):
        ev = self._deps(reads, writes)
        self._wait(e, ev, skip_sem=self.csem['pe'] if e == 'pe' else None)
        self.ninstr += 1
        if signal:
            self.ccnt[e] += 1
            self.prog[e].append(('i', fn, self.csem[e]))
            event = (self.csem[e], self.ccnt[e])
            self.pending[e] = False
        else:
            self.prog[e].append(('i', fn, None))
            event = (self.csem[e], self.ccnt[e] + 1)
            self.pending[e] = True
        self._record(reads, writes, event)

    def dma(self, q, out, in_, reads=(), writes=(), **kw):
        ev = self._deps(reads, writes)
        self._wait(q, ev)
        d = self.dq[q]
        i = d['n'] % self.NS
        rnd = d['n'] // self.NS
        sem = d['sems'][i]
        if rnd > 0:
            self._wait(q, {sem.num: (sem, 16 * rnd)})
        self.prog[q].append(('d', out, in_, kw, sem))
        self.ninstr += 1
        d['n'] += 1
        self._record(reads, writes, (sem, 16 * (rnd + 1)))

    def barrier(self):
        evs = {}
        for e, s in self.csem.items():
            assert not self.pending[e], e
            if self.ccnt[e] > 0:
                evs[s.num] = (s, self.ccnt[e])
        for q, d in self.dq.items():
            for i, s in enumerate(d['sems']):
                cnt = (d['n'] - i + self.NS - 1) // self.NS if d['n'] > i else 0
                if cnt > 0:
                    evs[s.num] = (s, 16 * cnt)
        for e in self.engs:
            self._wait(e, evs)
        self.buf.clear()

    def _replay(self, e, eng):
        for it in self.prog[e]:
            if it[0] == 'w':
                eng.wait_ge(it[1], it[2])
            elif it[0] == 'i':
                ins = it[1]()
                if it[2] is not None:
                    ins.then_inc(it[2], 1)
            else:
                eng.dma_start(out=it[1], in_=it[2], **it[3]).then_inc(it[4], 16)

    def finish(self):
        self.barrier()
        with self.nc.Block() as block:
            @block.sync
            def _(eng):
                self._replay('sp', eng)

            @block.tensor
            def _(eng):
                self._replay('pe', eng)

            @block.vector
            def _(eng):
                self._replay('dve', eng)

            @block.scalar
            def _(eng):
                self._replay('act', eng)

            @block.gpsimd
            def _(eng):
                self._replay('pool', eng)


class KB:
    def __init__(self, nc, debug=()):
        self.nc = nc
        self.debug = set(debug)
        self.fw = FW(nc)
        self.arena_bytes = 212000
        ar = nc.alloc_sbuf_tensor("arena", [128, self.arena_bytes], U8)
        self.base = nc.lookup_mloc(ar).addr
        self.pers = 0
        self.top = self.arena_bytes
        self.off = 0
        self.uid = 0
        self.psall = nc.alloc_psum_tensor("psall", [128, 8, 512], F32)
        self.ps = [self.psall[:, i, :] for i in range(8)]

    def sb(self, name, shape, dtype, persistent=False):
        nb = int(np.prod(shape[1:])) * (4 if dtype == F32 else 2)
        nb = (nb + 63) // 64 * 64
        self.uid += 1
        t = self.nc.alloc_sbuf_tensor_at(f"{name}_{self.uid}", list(shape), dtype, offset=self.base + self.off)
        self.off += nb
        assert self.off <= self.top, (name, self.off, self.top)
        if persistent:
            self.pers = self.off
        return t

    def sb_top(self, name, shape, dtype):
        nb = int(np.prod(shape[1:])) * (4 if dtype == F32 else 2)
        nb = (nb + 63) // 64 * 64
        self.top -= nb
        assert self.top >= self.off, (name, self.top, self.off)
        self.uid += 1
        return self.nc.alloc_sbuf_tensor_at(f"{name}_{self.uid}", list(shape), dtype, offset=self.base + self.top)

    def phase(self):
        self.fw.barrier()
        self.off = self.pers
        if getattr(self, 'top_release', False):
            self.top = self.arena_bytes
            self.top_release = False

    def dram(self, name, shape, dtype):
        kind = "ExternalOutput" if name in self.debug else "Internal"
        return self.nc.dram_tensor(name, list(shape), dtype, kind=kind).ap()


def act_engine_copy(kb, eng, out, in_):
    nc = kb.nc
    if eng == 'act':
        return lambda: nc.scalar.copy(out=out, in_=in_)
    if eng == 'dve':
        return lambda: nc.vector.tensor_copy(out=out, in_=in_)
    return lambda: nc.gpsimd.tensor_copy(out=out, in_=in_)


def ln_group(kb, tag, xt, kx, n, xh, kxh, st, mv, rs, eps):
    nc, fw = kb.nc, kb.fw
    for i in range(n):
        for j in range(2):
            fw.op('dve', lambda i=i, j=j: nc.vector.bn_stats(out=st[:, i, j, :], in_=xt[:, i, j * 512:(j + 1) * 512]),
                  reads=[(kx, i)], writes=[(tag + 'st', i, j)])
        fw.op('dve', lambda i=i: nc.vector.bn_aggr(out=mv[:, i, :], in_=st[:, i, :, :].rearrange("p a b -> p (a b)")),
              reads=[(tag + 'st', i, 0), (tag + 'st', i, 1)], writes=[(tag + 'mv', i)])
    fw.op('act', lambda: nc.scalar.activation(out=rs[:, 0:n], in_=mv[:, 0:n, 1], func=AF.Ln, bias=eps[:], scale=1.0),
          reads=[(tag + 'mv', i) for i in range(n)], writes=[tag + 'rs'])
    fw.op('act', lambda: nc.scalar.activation(out=rs[:, 0:n], in_=rs[:, 0:n], func=AF.Exp, scale=-0.5),
          reads=[tag + 'rs'], writes=[tag + 'rs'])
    for i in range(n):
        fw.op('dve', lambda i=i: nc.vector.tensor_scalar(out=xh[:, i, :], in0=xt[:, i, :], scalar1=mv[:, i, 0:1], scalar2=rs[:, i:i + 1],
                                                          op0=ALU.subtract, op1=ALU.mult),
              reads=[(kx, i), (tag + 'mv', i), tag + 'rs'], writes=[(kxh, i)])


def transpose_mod(kb, tag, xh, kxh, n, tok0, uT, kuT, cst, lidx, jshift, jscale, pbanks):
    nc, fw = kb.nc, kb.fw
    for i in range(n):
        nsel = 0 if (tok0 + i * 128) < NL else 1
        for g in range(2):
            pb = pbanks[(i * 2 + g) % len(pbanks)]
            pt = kb.ps[pb]
            for j in range(4):
                k = g * 4 + j
                fw.op('pe', lambda i=i, k=k, j=j, pt=pt: nc.tensor.transpose(out=pt[:, j * 128:(j + 1) * 128], in_=xh[:, i, k * 128:(k + 1) * 128], identity=cst['ident'][:]),
                      reads=[(kxh, i)], writes=[('ps', pb)])
            for j in range(4):
                k = g * 4 + j
                fw.op('act', lambda i=i, k=k, j=j, pt=pt, nsel=nsel: nc.scalar.activation(
                    out=uT[:, k, i * 128:(i + 1) * 128], in_=pt[:, j * 128:(j + 1) * 128], func=AF.Identity,
                    scale=cst['mod1p'][:, lidx, jscale * 8 + k, nsel:nsel + 1], bias=cst['modT'][:, lidx, jshift * 8 + k, nsel:nsel + 1]),
                    reads=[('ps', pb)], writes=[(kuT, i, k)])


def phase_consts(kb, io):
    nc, fw = kb.nc, kb.fw
    cst = {}
    cst['ident'] = kb.sb('ident', [128, 128], F32, True)
    cst['ones'] = kb.sb('ones', [128, 128], F32, True)
    cst['eps'] = kb.sb('eps', [128, 1], F32, True)
    cst['modT'] = kb.sb('modT', [128, 2, 48, 2], F32, True)
    cst['mod1p'] = kb.sb('mod1p', [128, 2, 48, 2], F32, True)
    fw.dma('sp', cst['ident'][:], io['ident'], writes=['ident'])
    fw.op('pool', lambda: nc.gpsimd.memset(cst['ones'][:], 1.0), writes=['ones'])
    fw.op('pool', lambda: nc.gpsimd.memset(cst['eps'][:], LN_EPS), writes=['eps'])
    return cst


def phase_mods(kb, io, cst, S):
    nc, fw = kb.nc, kb.fw
    modT, mod1p = cst['modT'], cst['mod1p']
    cv = kb.sb('cv', [128, 8, 2], F32)
    sv = kb.sb('sv', [128, 8, 2], F32)
    bT = kb.sb('bT', [128, 2, 48], F32)
    wm = [kb.sb(f'wm{i}', [128, 8, 512], F32) for i in range(4)]
    fw.dma('sp', cv[:], io['cvec'], writes=['cv'])
    fw.dma('sp', bT[:], io['b_modT'], writes=['bT'])
    fw.op('act', lambda: nc.scalar.activation(out=sv[:], in_=cv[:], func=AF.Silu), reads=['cv'], writes=['sv'])
    it = 0
    for l in range(2):
        wv = io['w_mod'][l].rearrange("(k p) n -> p k n", p=128)
        for cb in range(12):
            b = it % 4
            fw.dma(('sp', 'act', 'pool')[it % 3], wm[b][:], wv[:, :, cb * 512:(cb + 1) * 512], writes=[('wm', b)])
            pb = it % 2
            for ci in range(4):
                for k in range(8):
                    fw.op('pe', lambda b=b, ci=ci, k=k, pb=pb: nc.tensor.matmul(kb.ps[pb][:, ci * 2:ci * 2 + 2], lhsT=wm[b][:, k, ci * 128:(ci + 1) * 128],
                                                                                rhs=sv[:, k, :], start=(k == 0), stop=(k == 7)),
                          reads=[('wm', b), 'sv'], writes=[('ps', pb)], signal=(k == 7))
            for n in range(2):
                fw.op('dve', lambda l=l, cb=cb, n=n, pb=pb: nc.vector.tensor_tensor(
                    out=modT[:, l, cb * 4:(cb + 1) * 4, n], in0=kb.ps[pb][:, 0:8].rearrange("p (c n) -> p c n", n=2)[:, :, n],
                    in1=bT[:, l, cb * 4:(cb + 1) * 4], op=ALU.add),
                    reads=[('ps', pb), 'bT'], writes=[('modT', l, cb, n)])
            it += 1
    allmod = [('modT', l, cb, n) for l in range(2) for cb in range(12) for n in range(2)]
    fw.op('dve', lambda: nc.vector.tensor_scalar_add(out=mod1p[:], in0=modT[:], scalar1=1.0), reads=allmod, writes=['mod1p'])
    dg = [kb.sb(f'dg{i}', [128, 128], F32) for i in range(2)]
    gbt = [kb.sb(f'gbt{i}', [128, 1024], F32) for i in range(2)]
    it = 0
    gi = 0
    for l in range(2):
        for jj, j in enumerate((2, 5)):
            for n in range(2):
                g = gi % 2
                for k in range(8):
                    b = it % 2
                    pb = 2 + (it // 4) % 2
                    fw.op('dve', lambda b=b, l=l, j=j, k=k, n=n: nc.vector.tensor_scalar_mul(out=dg[b][:], in0=cst['ident'][:], scalar1=modT[:, l, j * 8 + k, n:n + 1]),
                          reads=['ident', 'mod1p'], writes=[('dg', b)])
                    fw.op('pe', lambda b=b, pb=pb, k=k: nc.tensor.matmul(kb.ps[pb][:, (k % 4) * 128:(k % 4 + 1) * 128], lhsT=cst['ones'][:], rhs=dg[b][:], start=True, stop=True),
                          reads=[('dg', b), 'ones'], writes=[('ps', pb)])
                    if k % 4 == 3:
                        fw.op('act', lambda g=g, pb=pb, k=k: nc.scalar.copy(out=gbt[g][:, (k // 4) * 512:(k // 4 + 1) * 512], in_=kb.ps[pb][:]),
                              reads=[('ps', pb)], writes=[('gbt', g, k // 4)])
                    it += 1
                fw.dma('sp', S['gb'][l, jj, n], gbt[g][:], reads=[('gbt', g, 0), ('gbt', g, 1)])
                gi += 1


def phase_A(kb, io, cst, S):
    nc, fw = kb.nc, kb.fw
    W = kb.sb('Wa', [128, 8, 4096], BF16)
    wv = io['a_win'].rearrange("(k p) n -> p k n", p=128)
    for k in range(8):
        fw.dma('pool', W[:, k, :], wv[:, k, :], writes=[('Wa', k)])
    Wk = [('Wa', k) for k in range(8)]
    xt = [kb.sb(f'xt{i}', [128, 4, 1024], F32) for i in range(2)]
    uT = [kb.sb(f'uT{i}', [128, 8, 512], BF16) for i in range(2)]
    rc = [kb.sb(f'rc{i}', [128, 512], F32) for i in range(2)]
    rs_ = [kb.sb(f'rs{i}', [128, 512], F32) for i in range(2)]
    st = kb.sb('st', [128, 4, 2, 6], F32)
    mv = kb.sb('mv', [128, 4, 2], F32)
    rstd = kb.sb('rstd', [128, 4], F32)
    t1 = [kb.sb(f't1{i}', [128, 512], F32) for i in range(2)]
    t2 = [kb.sb(f't2{i}', [128, 512], F32) for i in range(2)]
    ob = [kb.sb(f'ob{i}', [128, 512], BF16) for i in range(4)]
    vt = [kb.sb(f'vt{i}', [128, 768], BF16) for i in range(2)]
    ngroups = (NT + 511) // 512
    obi = 0
    pbi = 0
    vti = 0
    def prologueA(g):
        tok0 = g * 512
        ntok = min(512, NT - tok0)
        n = ntok // 128
        b = g % 2
        for i in range(n):
            fw.dma('sp', xt[b][:, i, :], io['xin'][tok0 + i * 128: tok0 + (i + 1) * 128, :], writes=[(('xt', b), i)])
        fw.dma('sp', rc[b][:, 0:ntok], io['ropeC'][:, tok0:tok0 + ntok], writes=[('rc', b)])
        fw.dma('sp', rs_[b][:, 0:ntok], io['ropeS'][:, tok0:tok0 + ntok], writes=[('rs', b)])
        ln_group(kb, 'A', xt[b], ('xt', b), n, xt[b], ('xt', b), st, mv, rstd, cst['eps'])
        transpose_mod(kb, 'A', xt[b], ('xt', b), n, tok0, uT[b], ('uT', b), cst, 0, 0, 1, (0, 1))

    prologueA(0)
    for g in range(ngroups):
        tok0 = g * 512
        ntok = min(512, NT - tok0)
        n = ntok // 128
        b = g % 2
        uk = [(('uT', b), i, k) for i in range(n) for k in range(8)]
        for kind in range(2):
            if kind == 1 and g + 1 < ngroups:
                prologueA(g + 1)
            for h in range(6):
                pa = 2 + pbi % 6
                pbi += 1
                pbb = 2 + pbi % 6
                pbi += 1
                c0 = kind * 1536 + h * 128
                for k in range(8):
                    fw.op('pe', lambda k=k, pa=pa, c0=c0, b=b, ntok=ntok: nc.tensor.matmul(kb.ps[pa][:, 0:ntok], lhsT=W[:, k, c0:c0 + 128], rhs=uT[b][:, k, 0:ntok], start=(k == 0), stop=(k == 7)),
                          reads=uk + Wk if k == 0 else [], writes=[('ps', pa)], signal=(k == 7))
                for k in range(8):
                    fw.op('pe', lambda k=k, pbb=pbb, c0=c0, b=b, ntok=ntok: nc.tensor.matmul(kb.ps[pbb][:, 0:ntok], lhsT=W[:, k, c0 + 768:c0 + 896], rhs=uT[b][:, k, 0:ntok], start=(k == 0), stop=(k == 7)),
                          reads=[], writes=[('ps', pbb)], signal=(k == 7))
                tb = obi % 2
                o = obi % 4
                obi += 1
                fw.op('dve', lambda pa=pa, tb=tb, b=b, ntok=ntok: nc.vector.tensor_tensor(out=t1[tb][:, 0:ntok], in0=kb.ps[pa][:, 0:ntok], in1=rc[b][:, 0:ntok], op=ALU.mult),
                      reads=[('ps', pa), ('rc', b)], writes=[('t1', tb)])
                fw.op('dve', lambda pbb=pbb, tb=tb, b=b, ntok=ntok: nc.vector.tensor_tensor(out=t2[tb][:, 0:ntok], in0=kb.ps[pbb][:, 0:ntok], in1=rs_[b][:, 0:ntok], op=ALU.mult),
                      reads=[('ps', pbb), ('rs', b)], writes=[('t2', tb)])
                fw.op('pool', lambda tb=tb, o=o, ntok=ntok: nc.gpsimd.tensor_tensor(out=ob[o][:, 0:ntok], in0=t1[tb][:, 0:ntok], in1=t2[tb][:, 0:ntok], op=ALU.add),
                      reads=[('t1', tb), ('t2', tb)], writes=[('ob', o)])
                dst = S['qT'] if kind == 0 else S['kT']
                fw.dma('sp', dst[h, :, tok0:tok0 + ntok], ob[o][:, 0:ntok], reads=[('ob', o)])
        for j in range(2):
            pa = 2 + pbi % 6
            pbi += 1
            c0 = 3072 + j * 128
            for k in range(8):
                fw.op('pe', lambda k=k, pa=pa, c0=c0, b=b, ntok=ntok: nc.tensor.matmul(kb.ps[pa][:, 0:ntok], lhsT=W[:, k, c0:c0 + 128], rhs=uT[b][:, k, 0:ntok], start=(k == 0), stop=(k == 7)),
                      reads=uk + Wk if k == 0 else [], writes=[('ps', pa)], signal=(k == 7))
            o = obi % 4
            obi += 1
            fw.op('act', lambda pa=pa, o=o, ntok=ntok: nc.scalar.copy(out=ob[o][:, 0:ntok], in_=kb.ps[pa][:, 0:ntok]), reads=[('ps', pa)], writes=[('ob', o)])
            fw.dma('sp', S['fT'][j * 128:(j + 1) * 128, tok0:tok0 + ntok], ob[o][:, 0:ntok], reads=[('ob', o)])
        for i in range(n):
            v = vti % 2
            vti += 1
            for half in range(2):
                pa = 2 + pbi % 6
                pbi += 1
                c0 = 3328 + half * 384
                for k in range(8):
                    fw.op('pe', lambda k=k, pa=pa, c0=c0, b=b, i=i: nc.tensor.matmul(kb.ps[pa][:, 0:384], lhsT=uT[b][:, k, i * 128:(i + 1) * 128], rhs=W[:, k, c0:c0 + 384], start=(k == 0), stop=(k == 7)),
                          reads=uk + Wk if k == 0 else [], writes=[('ps', pa)], signal=(k == 7))
                fw.op('act', lambda pa=pa, v=v, half=half: nc.scalar.copy(out=vt[v][:, half * 384:(half + 1) * 384], in_=kb.ps[pa][:, 0:384]),
                      reads=[('ps', pa)], writes=[('vt', v, half)])
            fw.dma('sp', S['v'][tok0 + i * 128: tok0 + (i + 1) * 128, :], vt[v][:], reads=[('vt', v, 0), ('vt', v, 1)])


IN_SPECS = dict(
    xin=([NT, D], F32), cvec=([128, 8, 2], F32), w_mod=([2, D, 6 * D], F32), b_modT=([128, 2, 48], F32),
    ln_g=([2, 2, D], F32), ln_b=([2, 2, D], F32), w_ff1=([2, D, DFF], F32), w_ff3=([2, D, DFF], F32), w_ff2=([2, DFF, D], F32),
    a_win=([D, 4096], F32), a_wout=([D, D], F32), lamv=([4, 64], F32), da_hg=([128], F32),
    ropeC=([128, NT], F32), ropeS=([128, NT], F32), ident=([128, 128], F32), tri=([2, 128, 128], F32),
    bdc=([2, 2, 128, 512], BF16), dftN=([NL, 2, NL], BF16), dftC=([NCX, 2, NCX], BF16),
    m_win=([D, 4096], F32), m_wout=([2048, D], F32), convw=([128, 16, 5], F32), convb=([128, 16], F32),
    bdq=([16, 128, 128], F32), bdk=([16, 128, 128], F32), bdv=([16, 128, 128], F32),
    wg=([6144, 32], F32), bg=([32], F32), skipT=([128, 16], F32), mhgT=([128, 16], F32),
)


def build(stop=99, debug=(), only=None):
    nc = bass.Bass("TRN2", target_bir_lowering=False)
    io = {k: nc.dram_tensor(k, list(sh), dt, kind="ExternalInput").ap() for k, (sh, dt) in IN_SPECS.items()}
    out = nc.dram_tensor("out", [NL, D], F32, kind="ExternalOutput").ap()
    kb = KB(nc, debug)
    S = {}
    S['gb'] = kb.dram('gb', [2, 2, 2, 128, D], F32)
    S['qT'] = kb.dram('qT', [6, 128, NT], BF16)
    S['kT'] = kb.dram('kT', [6, 128, NT], BF16)
    S['v'] = kb.dram('v', [NT, 768], BF16)
    S['fT'] = kb.dram('fT', [256, NT], BF16)
    S['an'] = kb.dram('an', [NT, 768], F32)
    S['yT'] = kb.dram('yT', [256, NT], BF16)
    S['h1'] = kb.dram('h1', [NT, D], F32)
    if 'h2in' in debug:
        S['h2'] = nc.dram_tensor('h2in', [NT, D], F32, kind="ExternalInput").ap()
    else:
        S['h2'] = kb.dram('h2', [NT, D], F32)
    S['xmT'] = kb.dram('xmT', [2048, NT], BF16)
    S['szT'] = kb.dram('szT', [2048, NL], BF16)
    S['qT1'] = kb.dram('qT1', [2048, NT], BF16)
    S['kT1'] = kb.dram('kT1', [2048, NT], BF16)
    S['xcT'] = kb.dram('xcT', [2048, NL], BF16)
    S['ktok'] = kb.dram('ktok', [NT, 2048], BF16)
    S['vtok'] = kb.dram('vtok', [NT, 2048], BF16)
    S['gT'] = kb.dram('gT', [32, NT], F32)
    S['yT1'] = kb.dram('yT1', [2048, NL], BF16)
    cst = phase_consts(kb, io)
    T = {}

    def ph_M3(kb, io, cst, S):
        T['pers0'] = kb.pers
        for nm in ('EQ', 'EKS', 'E', 'EKW'):
            T[nm] = kb.sb(nm, [128, NTILE, 16], F32, True)
        phase_M3(kb, io, cst, S, T)

    def ph_M4(kb, io, cst, S):
        if 'tabs' in debug:
            tabs = nc.dram_tensor("tabs", [4, 128, NTILE * 16], F32, kind="ExternalOutput").ap()
            for i, nm in enumerate(('EQ', 'EKS', 'E', 'EKW')):
                kb.fw.dma('sp', tabs[i], T[nm][:].rearrange("p a b -> p (a b)"), reads=[])
        phase_M4(kb, io, cst, S, T)

    def ph_D1b(kb, io, cst, S):
        kb.pers = T.get('pers0', kb.pers)
        kb.off = kb.pers
        phase_D1(kb, io, cst, S, 1)
    phases = [phase_mods, phase_A, phase_B, phase_C,
              lambda kb, io, cst, S: phase_D1(kb, io, cst, S, 0),
              lambda kb, io, cst, S: phase_D2(kb, io, cst, S, 0, S['h2']),
              phase_M1, phase_M2, ph_M3, ph_M4, ph_D1b,
              lambda kb, io, cst, S: phase_D2(kb, io, cst, S, 1, out)]
    order = list(only) if only is not None else list(range(min(stop, len(phases))))
    for i in order:
        kb.phase()
        phases[i](kb, io, cst, S)
    if 'modT' in debug:
        dm = nc.dram_tensor("modT_o", [128, 2 * 48 * 2], F32, kind="ExternalOutput").ap()
        kb.fw.dma('sp', dm, cst['modT'][:].rearrange("p a b c -> p (a b c)"), reads=[])
    kb.fw.finish()
    return nc


def rope_tables():
    rows = NL // 64
    row = np.repeat(np.arange(rows, dtype=np.float32), 64)
    col = np.tile(np.arange(64, dtype=np.float32), rows)
    inv_freq = (np.float32(10000.0) ** (-np.arange(16, dtype=np.float32) / np.float32(16))).astype(np.float32)
    ang = np.concatenate([row[:, None] * inv_freq, col[:, None] * inv_freq], -1).astype(np.float32)
    cs, sn = np.cos(ang).astype(np.float32), np.sin(ang).astype(np.float32)
    C = np.ones((128, NT), np.float32)
    Sg = np.zeros((128, NT), np.float32)
    for m in range(2):
        for half in range(2):
            p0 = m * 64 + half * 32
            C[p0:p0 + 32, :NL] = cs.T
            Sg[p0:p0 + 32, :NL] = (-sn.T if half == 0 else sn.T)
    return C, Sg


def dft_tables():
    def dft(n):
        idx = (np.arange(n, dtype=np.int64)[:, None] * np.arange(n, dtype=np.int64)[None, :]) % n
        ang = 2.0 * np.pi * idx.astype(np.float64) / n
        return np.cos(ang), np.sin(ang)
    bf = ml_dtypes.bfloat16
    cN, sN = dft(NL)
    dftN = np.stack([cN, -sN], 1).astype(bf)
    cC, sC = dft(NCX)
    dftC = np.stack([cC, -sC], 1).astype(bf)
    c64, s64 = dft(64)
    bdc = np.zeros((2, 2, 128, 2, 4, 64), np.float64)
    for v, ntok in enumerate((NL, NCX)):
        nrm = 1.0 / math.sqrt(ntok * 64)
        for j in range(2):
            for g2 in range(2):
                g = 2 * j + g2
                bdc[v, j, g2 * 64:(g2 + 1) * 64, 0, g, :] = c64 * nrm
                bdc[v, j, g2 * 64:(g2 + 1) * 64, 1, g, :] = s64 * nrm
    return dftN, dftC, bdc.reshape(2, 2, 128, 512).astype(bf)


_CONST_CACHE = {}


def prep_shared(inp):
    f32 = np.float32
    sh = {}
    sh['w_mod'] = np.ascontiguousarray(inp['w_mod'], f32)
    sh['b_modT'] = np.ascontiguousarray(inp['b_mod'].reshape(2, 48, 128).transpose(2, 0, 1), f32)
    for k in ('ln_g', 'ln_b', 'w_ff1', 'w_ff3', 'w_ff2'):
        sh[k] = np.ascontiguousarray(inp[k], f32)
    w = inp['a_w_in'][0]
    cols = {k: [] for k in ('q', 'qs', 'k', 'ks', 'v')}
    for h in range(6):
        b0 = h * 384
        for m in range(2):
            q0 = b0 + m * 64
            k0 = b0 + 128 + m * 64
            cols['q'] += list(range(q0, q0 + 64))
            cols['qs'] += list(range(q0 + 32, q0 + 64)) + list(range(q0, q0 + 32))
            cols['k'] += list(range(k0, k0 + 64))
            cols['ks'] += list(range(k0 + 32, k0 + 64)) + list(range(k0, k0 + 32))
        cols['v'] += list(range(b0 + 256, b0 + 384))
    order = cols['q'] + cols['qs'] + cols['k'] + cols['ks'] + list(range(2304, 2560)) + cols['v']
    sh['a_win'] = np.ascontiguousarray(w[:, order], f32)
    sh['a_wout'] = np.ascontiguousarray(inp['a_w_out'][0], f32)
    sh['lamv'] = np.ascontiguousarray(np.stack([inp['da_lq1'][0], inp['da_lk1'][0], inp['da_lq2'][0], inp['da_lk2'][0]]), f32)
    sh['da_hg'] = np.ascontiguousarray(inp['da_head_g'][0], f32)
    if not _CONST_CACHE:
        C, Sg = rope_tables()
        dftN, dftC, bdc = dft_tables()
        tri = np.stack([np.triu(np.ones((128, 128), f32)), np.tril(np.ones((128, 128), f32))])
        _CONST_CACHE.update(ropeC=C, ropeS=Sg, dftN=dftN, dftC=dftC, bdc=bdc, tri=tri, ident=np.eye(128, dtype=f32))
    sh.update(_CONST_CACHE)
    sh['m_win'] = np.ascontiguousarray(inp['m_w_in'][0], f32)
    sh['m_wout'] = np.ascontiguousarray(inp['m_w_out'][0], f32)
    sh['convw'] = np.ascontiguousarray(inp['m_conv_w'][0].reshape(5, 16, 128).transpose(2, 1, 0), f32)
    sh['convb'] = np.ascontiguousarray(inp['m_conv_b'][0].reshape(16, 128).T, f32)
    for nm, key in (('bdq', 'm_wq'), ('bdk', 'm_wk'), ('bdv', 'm_wv')):
        bd = np.zeros((16, 128, 128), f32)
        blk = inp[key][0].reshape(16, 32, 4, 4)
        for j in range(32):
            bd[:, 4 * j:4 * j + 4, 4 * j:4 * j + 4] = blk[:, j]
        sh[nm] = bd
    wg = np.concatenate([inp['m_w_ig'][0].transpose(1, 0, 2).reshape(6144, 16), inp['m_w_fg'][0].transpose(1, 0, 2).reshape(6144, 16)], 1)
    sh['wg'] = np.ascontiguousarray(wg, f32)
    sh['bg'] = np.ascontiguousarray(np.concatenate([inp['m_b_ig'][0].reshape(16), inp['m_b_fg'][0].reshape(16)]), f32)
    sh['skipT'] = np.ascontiguousarray(inp['m_skip'][0].reshape(16, 128).T, f32)
    sh['mhgT'] = np.ascontiguousarray(inp['m_head_g'][0].reshape(16, 128).T, f32)
    return sh


def prep_core(inp, b):
    f32 = np.float32
    xin = np.concatenate([inp['x'][b], inp['ctx'][b]], 0).astype(f32)
    cv = np.stack([inp['c'][b], inp['c_ctx']], 1).astype(f32)
    cvec = np.ascontiguousarray(cv.reshape(8, 128, 2).transpose(1, 0, 2))
    return dict(xin=np.ascontiguousarray(xin), cvec=cvec)


def phase_B(kb, io, cst, S):
    nc, fw = kb.nc, kb.fw
    lv = kb.sb('lv', [128, 4, 64], F32)
    pr = kb.sb('pr', [128, 2, 64], F32)
    a12 = kb.sb('a12', [128, 2], F32)
    lam = kb.sb('lam', [128, 1], F32)
    nlam = kb.sb('nlam', [128, 1], F32)
    fw.dma('sp', lv[:].rearrange("p a b -> p (a b)"), io['lamv'].rearrange("a b -> (a b)").partition_broadcast(128), writes=['lv'])
    fw.op('dve', lambda: nc.vector.tensor_tensor(out=pr[:, 0, :], in0=lv[:, 0, :], in1=lv[:, 1, :], op=ALU.mult), reads=['lv'], writes=[('pr', 0)])
    fw.op('dve', lambda: nc.vector.tensor_tensor(out=pr[:, 1, :], in0=lv[:, 2, :], in1=lv[:, 3, :], op=ALU.mult), reads=['lv'], writes=[('pr', 1)])
    fw.op('dve', lambda: nc.vector.reduce_sum(out=a12[:], in_=pr[:], axis=mybir.AxisListType.X), reads=[('pr', 0), ('pr', 1)], writes=['a12'])
    fw.op('act', lambda: nc.scalar.activation(out=a12[:], in_=a12[:], func=AF.Exp), reads=['a12'], writes=['a12'])
    fw.op('dve', lambda: nc.vector.tensor_tensor(out=lam[:], in0=a12[:, 0:1], in1=a12[:, 1:2], op=ALU.subtract), reads=['a12'], writes=['lam'])
    fw.op('dve', lambda: nc.vector.tensor_scalar(out=nlam[:], in0=lam[:], scalar1=LAM_INIT, scalar2=-1.0, op0=ALU.add, op1=ALU.mult), reads=['lam'], writes=['nlam'])

    qT = [kb.sb(f'qTh{i}', [128, NT], BF16) for i in range(2)]
    kT = [kb.sb(f'kTh{i}', [128, NT], BF16) for i in range(2)]
    V = [kb.sb(f'Vh{i}', [128, NTILE, 129], BF16) for i in range(2)]
    pt = [kb.sb(f'pt{i}', [128, 2, 512], BF16) for i in range(3)]
    rr = [kb.sb(f'rr{i}', [128, 4], F32) for i in range(2)]
    tt = [kb.sb(f'tt{i}', [128, 128], F32) for i in range(2)]
    ot = [kb.sb(f'ot{i}', [128, 4, 128], F32) for i in range(2)]
    for i in range(2):
        fw.op('pool', lambda i=i: nc.gpsimd.memset(V[i][:], 1.0), writes=[('V', i)])
    its = []
    for h in range(6):
        for qb in range(9):
            q0 = qb * 512
            nq = min(512, NT - q0)
            keys = list(range(NTILE)) if qb < 8 else [32, 33]
            for kt in keys:
                its.append(dict(h=h, hb=h % 2, qb=qb, q0=q0, nq=nq, nqi=nq // 128, kt=kt, first=(kt == keys[0]), last=(kt == keys[-1])))

    def loads(h):
        hb = h % 2
        fw.dma('sp', qT[hb][:], S['qT'][h], writes=[('qT', hb)])
        fw.dma('sp', kT[hb][:], S['kT'][h], writes=[('kT', hb)])
        fw.dma('pool', V[hb][:, :, 0:128], S['v'][:, h * 128:(h + 1) * 128].rearrange("(t p) c -> p t c", p=128), writes=[('V', hb)])

    def qk(it, n):
        pp = n % 2
        hb, kt, q0, nq = it['hb'], it['kt'], it['q0'], it['nq']
        for m in range(2):
            fw.op('pe', lambda m=m: nc.tensor.matmul(
                kb.ps[2 * pp + m][:, 0:nq], lhsT=kT[hb][m * 64:(m + 1) * 64, kt * 128:(kt + 1) * 128], rhs=qT[hb][m * 64:(m + 1) * 64, q0:q0 + nq], start=True, stop=True),
                reads=[('qT', hb), ('kT', hb)], writes=[('ps', 2 * pp + m)], signal=(m == 1))

    def ex(it, n):
        pp = n % 2
        pi = n % 3
        nq = it['nq']
        fw.op('act', lambda: nc.scalar.activation(out=pt[pi][:, :, 0:nq], in_=kb.psall[:, 2 * pp:2 * pp + 2, 0:nq], func=AF.Exp, scale=0.125),
              reads=[('ps', 2 * pp), ('ps', 2 * pp + 1)], writes=[('pt', pi)])

    def av(it, n):
        pi = n % 3
        hb, kt, nqi = it['hb'], it['kt'], it['nqi']
        for m in range(2):
            for qi in range(nqi):
                bank = 4 + m * 2 + qi // 2
                c0 = (qi % 2) * 256
                last = (m == 1 and qi == nqi - 1)
                fw.op('pe', lambda m=m, qi=qi, bank=bank, c0=c0: nc.tensor.matmul(
                    kb.ps[bank][:, c0:c0 + 129], lhsT=pt[pi][:, m, qi * 128:(qi + 1) * 128], rhs=V[hb][:, kt, :],
                    start=(it['first'] and qi % 2 == 0), stop=it['last'], skip_group_check=True),
                    reads=[('pt', pi), ('V', hb)], writes=[('ps', bank)], signal=last)

    blkc = [0]

    accs = [kb.sb(f'accs{i}', [128, 4, 512], F32) for i in range(2)]

    def epi(it):
        ob = blkc[0] % 2
        blkc[0] += 1
        h, q0, nq, nqi = it['h'], it['q0'], it['nq'], it['nqi']
        nbk = (nqi + 1) // 2
        for m in range(2):
            for bb in range(nbk):
                bank = 4 + m * 2 + bb
                fw.op('dve', lambda m=m, bb=bb, bank=bank: nc.vector.tensor_copy(out=accs[ob][:, m * 2 + bb, :], in_=kb.ps[bank][:]), reads=[('ps', bank)], writes=[('accs', ob, m * 2 + bb)])
        for qi in range(nqi):
            a0 = qi // 2
            a1 = 2 + qi // 2
            c0 = (qi % 2) * 256
            tb = qi % 2
            fw.op('dve', lambda a0=a0, a1=a1, c0=c0, tb=tb, qi=qi: nc.vector.reciprocal(out=rr[ob][:, 0:1], in_=accs[ob][:, a0, c0 + 128:c0 + 129]), reads=[('accs', ob, a0)], writes=[('rr', ob, 0)])
            fw.op('dve', lambda a0=a0, a1=a1, c0=c0, tb=tb, qi=qi: nc.vector.reciprocal(out=rr[ob][:, 1:2], in_=accs[ob][:, a1, c0 + 128:c0 + 129]), reads=[('accs', ob, a1)], writes=[('rr', ob, 1)])
            fw.op('dve', lambda a0=a0, a1=a1, c0=c0, tb=tb, qi=qi: nc.vector.tensor_tensor(out=rr[ob][:, 2:3], in0=rr[ob][:, 1:2], in1=nlam[:], op=ALU.mult), reads=[('rr', ob, 1), 'nlam'], writes=[('rr', ob, 2)])
            fw.op('dve', lambda a0=a0, a1=a1, c0=c0, tb=tb, qi=qi: nc.vector.tensor_scalar_mul(out=tt[tb][:], in0=accs[ob][:, a0, c0:c0 + 128], scalar1=rr[ob][:, 0:1]),
                  reads=[('accs', ob, a0), ('rr', ob, 0)], writes=[('tt', tb)])
            fw.op('dve', lambda a0=a0, a1=a1, c0=c0, tb=tb, qi=qi: nc.vector.scalar_tensor_tensor(out=ot[ob][:, qi, :], in0=accs[ob][:, a1, c0:c0 + 128], scalar=rr[ob][:, 2:3], in1=tt[tb][:], op0=ALU.mult, op1=ALU.add),
                  reads=[('accs', ob, a1), ('rr', ob, 2), ('tt', tb)], writes=[('ot', ob, qi)])
        fw.dma('sp', S['an'][q0:q0 + nq, h * 128:(h + 1) * 128].rearrange("(t p) c -> p t c", p=128), ot[ob][:, 0:nqi, :],
               reads=[('ot', ob, qi) for qi in range(nqi)])

    loads(0)
    qk(its[0], 0)
    for n, it in enumerate(its):
        if it['qb'] == 0 and it['first'] and it['h'] + 1 < 6:
            loads(it['h'] + 1)
        if n + 1 < len(its):
            qk(its[n + 1], n + 1)
        ex(it, n)
        av(it, n)
        if it['last']:
            epi(it)


def phase_C(kb, io, cst, S):
    nc, fw = kb.nc, kb.fw
    fTt = kb.sb('fTt', [128, 2, NT], BF16)
    bdc = kb.sb('bdc', [128, 2, 2, 512], BF16)
    Gt = kb.sb('Gt', [128, NTILE, 512], BF16)
    yTt = kb.sb('yTt', [128, 2, NT], BF16)
    dn = [kb.sb(f'dn{i}', [128, 2, 2048], BF16) for i in range(6)]
    dnc = kb.sb('dnc', [128, 2, 2, NCX], BF16)
    for j in range(2):
        fw.dma('sp', fTt[:, j, :], S['fT'][j * 128:(j + 1) * 128, :], writes=[('fTt', j)])
    fw.dma('sp', bdc[:], io['bdc'].rearrange("v j p n -> p v j n"), writes=['bdc'])
    fw.dma('sp', dnc[:], io['dftC'].rearrange("(t p) c n -> p t c n", p=128), writes=['dnc'])
    for t in range(NTILE):
        v = 0 if t < 32 else 1
        pb = t % 4
        for j in range(2):
            fw.op('pe', lambda t=t, j=j, v=v, pb=pb: nc.tensor.matmul(kb.ps[pb][:], lhsT=fTt[:, j, t * 128:(t + 1) * 128], rhs=bdc[:, v, j, :], start=(j == 0), stop=(j == 1)),
                  reads=[('fTt', 0), ('fTt', 1), 'bdc'], writes=[('ps', pb)], signal=(j == 1))
        fw.op('act' if t % 2 == 0 else 'dve', act_engine_copy(kb, 'act' if t % 2 == 0 else 'dve', Gt[:, t, :], kb.ps[pb][:]), reads=[('ps', pb)], writes=[('Gt', t)])
    kb.fw.barrier()
    di = 0
    for p in range(2):
        for t in range(32):
            b = di % 6
            di += 1
            fw.dma(('sp', 'pool', 'act')[di % 3], dn[b][:], io['dftN'][t * 128:(t + 1) * 128, :, p * 2048:(p + 1) * 2048], writes=[('dn', b)])
            cnt = 0
            for cs in range(2):
                for jj in range(2):
                    for nb in range(4):
                        cnt += 1
                        fw.op('pe', lambda t=t, cs=cs, jj=jj, nb=nb, b=b: nc.tensor.matmul(kb.ps[jj * 4 + nb][:], lhsT=Gt[:, t, cs * 256 + jj * 128: cs * 256 + (jj + 1) * 128],
                                                                                           rhs=dn[b][:, cs, nb * 512:(nb + 1) * 512], start=(t == 0 and cs == 0), stop=(t == 31 and cs == 1)),
                              reads=[('dn', b)], writes=[('ps', jj * 4 + nb)], signal=(cnt == 16))
        for jj in range(2):
            for nb in range(4):
                e = 'act' if nb % 2 == 0 else 'dve'
                c0 = p * 2048 + nb * 512
                fw.op(e, act_engine_copy(kb, e, yTt[:, jj, c0:c0 + 512], kb.ps[jj * 4 + nb][:]), reads=[('ps', jj * 4 + nb)], writes=[('yTt', jj, p, nb)])
    for jj in range(2):
        cnt = 0
        for tl in range(2):
            for cs in range(2):
                cnt += 1
                fw.op('pe', lambda jj=jj, tl=tl, cs=cs: nc.tensor.matmul(kb.ps[jj][:, 0:NCX], lhsT=Gt[:, 32 + tl, cs * 256 + jj * 128: cs * 256 + (jj + 1) * 128],
                                                                          rhs=dnc[:, tl, cs, :], start=(tl == 0 and cs == 0), stop=(tl == 1 and cs == 1)),
                      reads=['dnc'], writes=[('ps', jj)], signal=(cnt == 4))
        fw.op('act', act_engine_copy(kb, 'act', yTt[:, jj, NL:NT], kb.ps[jj][:, 0:NCX]), reads=[('ps', jj)], writes=[('yTt', jj, 9, 9)])
    kb.fw.barrier()
    for jj in range(2):
        fw.dma('sp', S['yT'][jj * 128:(jj + 1) * 128, :], yTt[:, jj, :], reads=[])


def post_norm_tiles(kb, tag, n, ybanks_of, hres, khres, gbt_of, zt, kz, st, mv, rs, lng, lnb, eps, dst_of, tiles):
    nc, fw = kb.nc, kb.fw
    for i in range(n):
        gb = gbt_of(i)
        for nb in range(2):
            pb = ybanks_of(i, nb)
            fw.op('dve', lambda i=i, nb=nb, pb=pb, gb=gb: nc.vector.tensor_tensor(out=zt[:, i, nb * 512:(nb + 1) * 512], in0=kb.ps[pb][:], in1=gb[:, nb * 512:(nb + 1) * 512], op=ALU.mult),
                  reads=[('ps', pb), 'gbt'], writes=[(kz, i)])
        fw.op('dve', lambda i=i: nc.vector.scalar_tensor_tensor(out=zt[:, i, :], in0=hres[:, i, :], scalar=ALPHA, in1=zt[:, i, :], op0=ALU.mult, op1=ALU.add),
              reads=[(khres, i)], writes=[(kz, i)])
    ln_group(kb, tag + 'pn', zt, kz, n, zt, kz, st, mv, rs, eps)
    for i in range(n):
        fw.op('pool', lambda i=i: nc.gpsimd.tensor_tensor(out=zt[:, i, :], in0=zt[:, i, :], in1=lng[:], op=ALU.mult), reads=['lng'], writes=[(kz, i)])
        fw.op('pool', lambda i=i: nc.gpsimd.tensor_tensor(out=zt[:, i, :], in0=zt[:, i, :], in1=lnb[:], op=ALU.add), reads=['lnb'], writes=[(kz, i)])
        fw.dma('sp', dst_of(tiles[i]), zt[:, i, :], reads=[(kz, i)])


def phase_D1(kb, io, cst, S, l):
    nc, fw = kb.nc, kb.fw
    KC = 8 if l == 0 else 16
    ntile = NTILE if l == 0 else NL // 128
    Wo = kb.sb('Wo', [128, KC, D], BF16)
    wsrc = (io['a_wout'] if l == 0 else io['m_wout']).rearrange("(k p) n -> p k n", p=128)
    for k in range(KC):
        fw.dma('pool', Wo[:, k, :], wsrc[:, k, :], writes=[('Wo', k)])
    gbt = [kb.sb(f'gbt{n}', [128, D], F32) for n in range(2)]
    for n in range(2):
        fw.dma('sp', gbt[n][:], S['gb'][l, 0, n], writes=['gbt'])
    lng = kb.sb('lng', [128, D], F32)
    lnb = kb.sb('lnb', [128, D], F32)
    fw.dma('sp', lng[:], io['ln_g'][l, 0].partition_broadcast(128), writes=['lng'])
    fw.dma('sp', lnb[:], io['ln_b'][l, 0].partition_broadcast(128), writes=['lnb'])
    mT = [kb.sb(f'mT{i}', [128, KC, 128], BF16) for i in range(4)]
    ht = [kb.sb(f'ht{i}', [128, 2, D], F32) for i in range(2)]
    zt = [kb.sb(f'zt{i}', [128, 2, D], F32) for i in range(2)]
    st = kb.sb('st', [128, 2, 2, 6], F32)
    mv = kb.sb('mv', [128, 2, 2], F32)
    rs = kb.sb('rs', [128, 2], F32)
    if l == 0:
        an = [kb.sb(f'an{i}', [128, 768], F32) for i in range(2)]
        sq = kb.sb('sq', [128, 768], F32)
        ss = [kb.sb(f'ss{i}', [128, 6], F32) for i in range(2)]
        hgb = kb.sb('hgb', [128, 128], F32)
        fw.dma('sp', hgb[:], io['da_hg'].partition_broadcast(128), writes=['hgb'])
        fw.op('dve', lambda: nc.vector.tensor_scalar_mul(out=hgb[:], in0=hgb[:], scalar1=1.0 - LAM_INIT), reads=['hgb'], writes=['hgb'])
    hsrc = io['xin'] if l == 0 else S['h2']
    W1p = kb.sb_top('W1p', [128, 8, DFF], BF16)
    W3p = kb.sb_top('W3p', [128, 8, DFF], BF16)
    w1s = io['w_ff1'][l].rearrange("(k p) n -> p k n", p=128)
    w3s = io['w_ff3'][l].rearrange("(k p) n -> p k n", p=128)
    for k in range(8):
        fw.dma('pool', W1p[:, k, :], w1s[:, k, :], writes=[('W1p', k)])
        fw.dma('pool', W3p[:, k, :], w3s[:, k, :], writes=[('W3p', k)])
    S['ffw'] = (W1p, W3p)
    st2 = kb.sb('st2', [128, 2, 2, 6], F32)
    mv2 = kb.sb('mv2', [128, 2, 2], F32)
    rs2 = kb.sb('rs2', [128, 2], F32)
    ng = ntile // 2
    info = {}

    def prologue(g):
        gbuf = g % 2
        for i in range(2):
            t = g * 2 + i
            ti = t
            mb = ti % 4
            ab = ti % 2
            fw.dma('sp', ht[gbuf][:, i, :], hsrc[t * 128:(t + 1) * 128, :], writes=[(('ht', gbuf), i)])
            if l == 0:
                fw.dma('sp', an[ab][:], S['an'][t * 128:(t + 1) * 128, :], writes=[('an', ab)])
                fw.dma('sp', mT[mb][:, 6:8, :], S['yT'][:, t * 128:(t + 1) * 128].rearrange("(k p) n -> p k n", p=128), writes=[('mT', mb, 'f')])
                fw.op('dve', lambda ab=ab: nc.vector.tensor_tensor(out=sq[:], in0=an[ab][:], in1=an[ab][:], op=ALU.mult), reads=[('an', ab)], writes=['sq'])
                fw.op('dve', lambda ab=ab: nc.vector.reduce_sum(out=ss[ab][:], in_=sq[:].rearrange("p (h c) -> p h c", h=6), axis=mybir.AxisListType.X), reads=['sq'], writes=[('ss', ab)])
                fw.op('act', lambda ab=ab: nc.scalar.activation(out=ss[ab][:], in_=ss[ab][:], func=AF.Ln, bias=cst['eps'][:], scale=1.0 / 128), reads=[('ss', ab)], writes=[('ss', ab)])
                fw.op('act', lambda ab=ab: nc.scalar.activation(out=ss[ab][:], in_=ss[ab][:], func=AF.Exp, scale=-0.5), reads=[('ss', ab)], writes=[('ss', ab)])
                for h in range(6):
                    fw.op('dve', lambda ab=ab, h=h: nc.vector.scalar_tensor_tensor(out=an[ab][:, h * 128:(h + 1) * 128], in0=an[ab][:, h * 128:(h + 1) * 128], scalar=ss[ab][:, h:h + 1], in1=hgb[:],
                                                                                   op0=ALU.mult, op1=ALU.mult), reads=[('an', ab), ('ss', ab), 'hgb'], writes=[('an', ab)])
                for grp in range(2):
                    pb = grp
                    for j in range(3):
                        k = grp * 3 + j
                        fw.op('pe', lambda ab=ab, k=k, j=j, pb=pb: nc.tensor.transpose(out=kb.ps[pb][:, j * 128:(j + 1) * 128], in_=an[ab][:, k * 128:(k + 1) * 128], identity=cst['ident'][:]),
                              reads=[('an', ab)], writes=[('ps', pb)])
                    fw.op('act', lambda mb=mb, grp=grp, pb=pb: nc.scalar.copy(out=mT[mb][:, grp * 3:(grp + 1) * 3, :], in_=kb.ps[pb][:, 0:384].rearrange("p (k n) -> p k n", k=3)),
                          reads=[('ps', pb)], writes=[('mT', mb, grp)])
                info[t] = (mb, [('mT', mb, 0), ('mT', mb, 1), ('mT', mb, 'f')])
            else:
                fw.dma('sp', mT[mb][:], S['yT1'][:, t * 128:(t + 1) * 128].rearrange("(k p) n -> p k n", p=128), writes=[('mT', mb, 0)])
                info[t] = (mb, [('mT', mb, 0)])

    def mm(g):
        for i in range(2):
            t = g * 2 + i
            mb, mkeys = info[t]
            for nb in range(2):
                pb = 2 + i * 2 + nb
                for k in range(KC):
                    fw.op('pe', lambda mb=mb, k=k, nb=nb, pb=pb: nc.tensor.matmul(kb.ps[pb][:], lhsT=mT[mb][:, k, :], rhs=Wo[:, k, nb * 512:(nb + 1) * 512], start=(k == 0), stop=(k == KC - 1)),
                          reads=(mkeys + [('Wo', kk) for kk in range(KC)]) if k == 0 else [], writes=[('ps', pb)], signal=(k == KC - 1))

    prologue(0)
    for g in range(ng):
        gbuf = g % 2
        mm(g)
        if g + 1 < ng:
            prologue(g + 1)
        tiles = [g * 2, g * 2 + 1]
        post_norm_tiles(kb, 'D1', 2, lambda i, nb: 2 + i * 2 + nb, ht[gbuf], ('ht', gbuf), lambda i, g=g: gbt[0 if (g * 2 + i) < 32 else 1], zt[gbuf], ('zt', gbuf), st2, mv2, rs2, lng, lnb, cst['eps'],
                        lambda t: S['h1'][t * 128:(t + 1) * 128, :], tiles)


def phase_D2(kb, io, cst, S, l, dst):
    nc, fw = kb.nc, kb.fw
    ntile = NTILE if l == 0 else NL // 128
    NF = DFF // 128
    W1, W3 = S.pop('ffw')
    W2 = kb.sb('W2', [128, NF, D], BF16)
    w2s = io['w_ff2'][l].rearrange("(k p) n -> p k n", p=128)
    for k in range(NF):
        fw.dma('pool', W2[:, k, :], w2s[:, k, :], writes=[('W2', k)])
    wkeys = [('W1', k) for k in range(8)] + [('W3', k) for k in range(8)]
    w2keys = [('W2', k) for k in range(NF)]
    ngb = 2 if l == 0 else 1
    gbt = [kb.sb(f'gbt{n}', [128, D], F32) for n in range(ngb)]
    for n in range(ngb):
        fw.dma('sp', gbt[n][:], S['gb'][l, 1, n], writes=['gbt'])
    lng = kb.sb('lng', [128, D], F32)
    lnb = kb.sb('lnb', [128, D], F32)
    fw.dma('sp', lng[:], io['ln_g'][l, 1].partition_broadcast(128), writes=['lng'])
    fw.dma('sp', lnb[:], io['ln_b'][l, 1].partition_broadcast(128), writes=['lnb'])
    ht = [kb.sb(f'ht{i}', [128, 2, D], F32) for i in range(2)]
    xh = kb.sb('xh', [128, 2, D], F32)
    uT = kb.sb('uT', [128, 8, 256], BF16)
    gT = kb.sb('gT', [128, NF, 256], BF16)
    sil = [kb.sb(f'sil{i}', [128, 256], F32) for i in range(2)]
    zt = kb.sb('zt', [128, 2, D], F32)
    st = kb.sb('st', [128, 2, 2, 6], F32)
    mv = kb.sb('mv', [128, 2, 2], F32)
    rs = kb.sb('rs', [128, 2], F32)
    st2 = kb.sb('st2', [128, 2, 2, 6], F32)
    mv2 = kb.sb('mv2', [128, 2, 2], F32)
    rs2 = kb.sb('rs2', [128, 2], F32)
    ng = ntile // 2

    def prologue(g):
        gbuf = g % 2
        tok0 = g * 256
        for i in range(2):
            fw.dma('sp', ht[gbuf][:, i, :], S['h1'][tok0 + i * 128: tok0 + (i + 1) * 128, :], writes=[(('ht', gbuf), i)])
        ln_group(kb, 'F', ht[gbuf], ('ht', gbuf), 2, xh, 'xh', st, mv, rs, cst['eps'])
        transpose_mod(kb, 'F', xh, 'xh', 2, tok0, uT, 'uT', cst, l, 3, 4, (0, 1))

    sic = [0]

    def up(g):
        uk = [('uT', i, k) for i in range(2) for k in range(8)]
        for fc in range(NF):
            pb = 2 + fc % 2
            for wi, Wm in enumerate((W1, W3)):
                for k in range(8):
                    fw.op('pe', lambda Wm=Wm, wi=wi, k=k, fc=fc, pb=pb: nc.tensor.matmul(kb.ps[pb][:, wi * 256:(wi + 1) * 256], lhsT=Wm[:, k, fc * 128:(fc + 1) * 128], rhs=uT[:, k, :],
                                                                                          start=(k == 0), stop=(k == 7)),
                          reads=(uk + wkeys) if (k == 0 and wi == 0) else [], writes=[('ps', pb)], signal=(k == 7 and wi == 1))
            sb_ = sic[0] % 2
            sic[0] += 1
            fw.op('act', lambda pb=pb, sb_=sb_: nc.scalar.activation(out=sil[sb_][:], in_=kb.ps[pb][:, 0:256], func=AF.Silu), reads=[('ps', pb)], writes=[('sil', sb_)])
            fw.op('dve', lambda pb=pb, sb_=sb_, fc=fc: nc.vector.tensor_tensor(out=gT[:, fc, :], in0=sil[sb_][:], in1=kb.ps[pb][:, 256:512], op=ALU.mult),
                  reads=[('ps', pb), ('sil', sb_)], writes=[('gT', fc)])

    def down(g):
        gk = [('gT', fc) for fc in range(NF)]
        for i in range(2):
            for nb in range(2):
                pb = 4 + i * 2 + nb
                for fc in range(NF):
                    fw.op('pe', lambda i=i, nb=nb, fc=fc, pb=pb: nc.tensor.matmul(kb.ps[pb][:], lhsT=gT[:, fc, i * 128:(i + 1) * 128], rhs=W2[:, fc, nb * 512:(nb + 1) * 512],
                                                                                  start=(fc == 0), stop=(fc == NF - 1)),
                          reads=(gk + w2keys) if fc == 0 else [], writes=[('ps', pb)], signal=(fc == NF - 1))

    kb.top_release = True
    prologue(0)
    for g in range(ng):
        gbuf = g % 2
        up(g)
        if g + 1 < ng:
            prologue(g + 1)
        down(g)
        tiles = [g * 2, g * 2 + 1]
        post_norm_tiles(kb, 'D2', 2, lambda i, nb: 4 + i * 2 + nb, ht[gbuf], ('ht', gbuf), lambda i, g=g: gbt[0 if (g * 2 + i) < 32 else 1], zt, 'zt', st2, mv2, rs2, lng, lnb, cst['eps'],
                        lambda t: dst[t * 128:(t + 1) * 128, :], tiles)


def phase_M1(kb, io, cst, S):
    nc, fw = kb.nc, kb.fw
    W = kb.sb('Wm', [128, 8, 4096], BF16)
    wv = io['m_win'].rearrange("(k p) n -> p k n", p=128)
    for k in range(8):
        fw.dma('pool', W[:, k, :], wv[:, k, :], writes=[('Wm', k)])
    Wk = [('Wm', k) for k in range(8)]
    xt = [kb.sb(f'xt{i}', [128, 4, 1024], F32) for i in range(2)]
    uT = [kb.sb(f'uT{i}', [128, 8, 512], BF16) for i in range(2)]
    st = kb.sb('st', [128, 4, 2, 6], F32)
    mv = kb.sb('mv', [128, 4, 2], F32)
    rstd = kb.sb('rstd', [128, 4], F32)
    ob = [kb.sb(f'ob{i}', [128, 4, 512], BF16) for i in range(3)]
    ngroups = (NT + 511) // 512
    obi = 0
    pbi = 0
    def prologueM(g):
        tok0 = g * 512
        ntok = min(512, NT - tok0)
        n = ntok // 128
        b = g % 2
        for i in range(n):
            fw.dma('sp', xt[b][:, i, :], S['h2'][tok0 + i * 128: tok0 + (i + 1) * 128, :], writes=[(('xt', b), i)])
        ln_group(kb, 'M', xt[b], ('xt', b), n, xt[b], ('xt', b), st, mv, rstd, cst['eps'])
        transpose_mod(kb, 'M', xt[b], ('xt', b), n, tok0, uT[b], ('uT', b), cst, 1, 0, 1, (0, 1))

    prologueM(0)
    for g in range(ngroups):
        tok0 = g * 512
        ntok = min(512, NT - tok0)
        n = ntok // 128
        b = g % 2
        uk = [(('uT', b), i, k) for i in range(n) for k in range(8)]
        ncg = 8 if tok0 < NL else 4
        for cg in range(ncg):
            if cg == 2 and g + 1 < ngroups:
                prologueM(g + 1)
            o = obi % 3
            obi += 1
            for cc in range(4):
                c = cg * 4 + cc
                pa = 2 + pbi % 6
                pbi += 1
                for k in range(8):
                    fw.op('pe', lambda k=k, pa=pa, c=c, b=b, ntok=ntok: nc.tensor.matmul(kb.ps[pa][:, 0:ntok], lhsT=W[:, k, c * 128:(c + 1) * 128], rhs=uT[b][:, k, 0:ntok], start=(k == 0), stop=(k == 7)),
                          reads=uk + Wk if k == 0 else [], writes=[('ps', pa)], signal=(k == 7))
                if c < 16:
                    e = 'dve' if cc % 2 == 0 else 'act'
                    fw.op(e, act_engine_copy(kb, e, ob[o][:, cc, 0:ntok], kb.ps[pa][:, 0:ntok]), reads=[('ps', pa)], writes=[('ob', o, cc)])
                else:
                    fw.op('act', lambda pa=pa, o=o, cc=cc, ntok=ntok: nc.scalar.activation(out=ob[o][:, cc, 0:ntok], in_=kb.ps[pa][:, 0:ntok], func=AF.Silu), reads=[('ps', pa)], writes=[('ob', o, cc)])
            if cg < 4:
                dst = S['xmT'][cg * 512:(cg + 1) * 512, tok0:tok0 + ntok]
            else:
                dst = S['szT'][(cg - 4) * 512:(cg - 3) * 512, tok0:tok0 + ntok]
            fw.dma('sp', dst.rearrange("(c p) n -> p c n", p=128), ob[o][:, :, 0:ntok], reads=[('ob', o, cc) for cc in range(4)])


def phase_M2(kb, io, cst, S):
    nc, fw = kb.nc, kb.fw
    cw = kb.sb('cw', [128, 16, 5], F32)
    cb = kb.sb('cb', [128, 16], F32)
    fw.dma('sp', cw[:], io['convw'], writes=['cw'])
    fw.dma('sp', cb[:], io['convb'], writes=['cb'])
    dgw = kb.sb('dgw', [128, 16, 5, 128], BF16)
    for c in range(16):
        for j in range(5):
            fw.op('dve', lambda c=c, j=j: nc.vector.tensor_scalar_mul(out=dgw[:, c, j, :], in0=cst['ident'][:], scalar1=cw[:, c, j:j + 1]), reads=['cw'], writes=[('dgw', c, j)])
    bd = {}
    for nm in ('bdq', 'bdk', 'bdv'):
        bd[nm] = kb.sb(nm, [128, 16, 128], BF16)
        fw.dma('pool', bd[nm][:], io[nm].rearrange("c p n -> p c n"), writes=[nm])
    Wg = kb.sb('Wg', [128, 48, 32], BF16)
    fw.dma('pool', Wg[:], io['wg'].rearrange("(c p) n -> p c n", p=128), writes=['Wg'])
    BS = 256
    NB = BS // 128
    xm = [kb.sb(f'xm{i}', [128, 16, BS + 4], BF16) for i in range(2)]
    xc = [kb.sb(f'xc{i}', [128, 16, BS], BF16) for i in range(2)]
    qf = [kb.sb(f'qf{i}', [128, 16, BS], BF16) for i in range(2)]
    kf = [kb.sb(f'kf{i}', [128, 16, BS], BF16) for i in range(2)]
    vf = [kb.sb(f'vf{i}', [128, 16, BS], BF16) for i in range(2)]
    ktk = [kb.sb(f'ktk{i}', [128, NB, 2048], BF16) for i in range(2)]
    vtk = [kb.sb(f'vtk{i}', [128, NB, 2048], BF16) for i in range(2)]
    gsb = [kb.sb(f'gsb{i}', [32, BS], F32) for i in range(2)]
    kb.fw.barrier()
    pbc = [0]

    def nextbank():
        pa = pbc[0] % 7
        pbc[0] += 1
        return pa

    nblk = NT // BS
    nlat = NL // BS
    for blk in range(nblk):
        tok0 = blk * BS
        ntok = BS
        b = blk % 2
        lat = blk < nlat
        fw.op('pool', lambda b=b: nc.gpsimd.memset(xm[b][:], 0.0), writes=[('xm', b)])
        lo = tok0 - 2 if (lat and blk > 0) else tok0
        hi = tok0 + ntok + 2 if (lat and blk < nlat - 1) else tok0 + ntok
        fw.dma('sp', xm[b][:, :, 2 - (tok0 - lo): 2 + (hi - tok0)], S['xmT'][:, lo:hi].rearrange("(c p) n -> p c n", p=128), writes=[('xm', b)])

        def conv(c, b=b):
            pa = nextbank()
            for j in range(5):
                fw.op('pe', lambda c=c, j=j, pa=pa: nc.tensor.matmul(kb.ps[pa][:, 0:ntok], lhsT=dgw[:, c, j, :], rhs=xm[b][:, c, j:j + ntok], start=(j == 0), stop=(j == 4)),
                      reads=[('xm', b)] + [('dgw', c, jj) for jj in range(5)] if j == 0 else [], writes=[('ps', pa)], signal=(j == 4))
            fw.op('act', lambda c=c, pa=pa: nc.scalar.activation(out=xc[b][:, c, :], in_=kb.ps[pa][:, 0:ntok], func=AF.Silu, bias=cb[:, c:c + 1], scale=1.0),
                  reads=[('ps', pa), 'cb'], writes=[('xc', b, c)])

        def qkv(c, b=b):
            for nm, src, dstt, kk in (('bdq', 'xc', qf[b], 'qf'), ('bdk', 'xc', kf[b], 'kf'), ('bdv', 'xm', vf[b], 'vf')):
                pa = nextbank()
                rhs = xc[b][:, c, :] if src == 'xc' else xm[b][:, c, 2:2 + ntok]
                fw.op('pe', lambda nm=nm, c=c, pa=pa, rhs=rhs: nc.tensor.matmul(kb.ps[pa][:, 0:ntok], lhsT=bd[nm][:, c, :], rhs=rhs, start=True, stop=True),
                      reads=[nm, ('xc', b, c) if src == 'xc' else ('xm', b)], writes=[('ps', pa)])
                e = 'dve' if kk != 'kf' else 'act'
                fw.op(e, act_engine_copy(kb, e, dstt[:, c, :], kb.ps[pa][:, 0:ntok]), reads=[('ps', pa)], writes=[(kk, b, c)])

        def gates(c, b=b):
            for gi, (src, kk) in enumerate(((qf[b], 'qf'), (kf[b], 'kf'), (vf[b], 'vf'))):
                fw.op('pe', lambda gi=gi, c=c, src=src: nc.tensor.matmul(kb.ps[7][0:32, 0:ntok], lhsT=Wg[:, gi * 16 + c, :], rhs=src[:, c, :],
                                                                        start=(c == 0 and gi == 0), stop=(c == 15 and gi == 2)),
                      reads=[(kk, b, c), 'Wg'], writes=[('ps', 7)], signal=(c == 15 and gi == 2))

        conv(0)
        qkv(0)
        for c in range(16):
            if c + 1 < 16:
                conv(c + 1)
            gates(c)
            if c + 1 < 16:
                qkv(c + 1)
        fw.op('dve', lambda b=b: nc.vector.tensor_copy(out=gsb[b][:], in_=kb.ps[7][0:32, 0:ntok]), reads=[('ps', 7)], writes=[('gsb', b)])
        fw.dma('sp', S['gT'][:, tok0:tok0 + ntok], gsb[b][:], reads=[('gsb', b)])
        for which, srct, bdn, dstt, kk in ((0, xc[b], 'bdk', ktk[b], 'ktk'), (1, xm[b], 'bdv', vtk[b], 'vtk')):
            for i in range(NB):
                for cg in range(4):
                    pa = nextbank()
                    for cc in range(4):
                        c = cg * 4 + cc
                        off = 0 if which == 0 else 2
                        fw.op('pe', lambda srct=srct, c=c, cc=cc, i=i, pa=pa, bdn=bdn, off=off: nc.tensor.matmul(
                            kb.ps[pa][:, cc * 128:(cc + 1) * 128], lhsT=srct[:, c, off + i * 128: off + (i + 1) * 128], rhs=bd[bdn][:, c, :], start=True, stop=True),
                            reads=[('xc', b, c) if which == 0 else ('xm', b), bdn], writes=[('ps', pa)], signal=(cc == 3))
                    e = 'act' if (cg % 2 == 0) else 'dve'
                    fw.op(e, act_engine_copy(kb, e, dstt[:, i, cg * 512:(cg + 1) * 512], kb.ps[pa][:]), reads=[('ps', pa)], writes=[(kk, b, i, cg)])
                dst = S['ktok'] if which == 0 else S['vtok']
                fw.dma('sp' if which == 0 else 'act', dst[tok0 + i * 128: tok0 + (i + 1) * 128, :], dstt[:, i, :], reads=[(kk, b, i, cg) for cg in range(4)])
        allc = lambda kk: [(kk, b, c) for c in range(16)]
        fw.dma('sp', S['qT1'][:, tok0:tok0 + ntok].rearrange("(c p) n -> p c n", p=128), qf[b][:], reads=allc('qf'))
        fw.dma('act', S['kT1'][:, tok0:tok0 + ntok].rearrange("(c p) n -> p c n", p=128), kf[b][:], reads=allc('kf'))
        if lat:
            fw.dma('sp', S['xcT'][:, tok0:tok0 + ntok].rearrange("(c p) n -> p c n", p=128), xc[b][:], reads=allc('xc'))


def phase_M3(kb, io, cst, S, T):
    nc, fw = kb.nc, kb.fw
    gsb = kb.sb('gsb', [32, NT], F32)
    bgc = kb.sb('bgc', [32, 1], F32)
    G = kb.sb('G', [128, NTILE, 32], F32)
    NLF = kb.sb('NLF', [128, NTILE, 16], F32)
    tri = kb.sb('tri', [128, 2, 128], F32)
    one = kb.sb('one', [128, 1], F32)
    A = kb.sb('A', [128, NTILE, 16], F32)
    fw.dma('sp', gsb[:], S['gT'], writes=['gsb'])
    fw.dma('sp', bgc[:], io['bg'].rearrange("(p o) -> p o", o=1), writes=['bgc'])
    fw.dma('sp', tri[:], io['tri'].rearrange("d p n -> p d n"), writes=['tri'])
    fw.op('pool', lambda: nc.gpsimd.memset(one[:], 1.0), writes=['one'])
    fw.op('dve', lambda: nc.vector.tensor_scalar_add(out=gsb[:], in0=gsb[:], scalar1=bgc[:]), reads=['gsb', 'bgc'], writes=['gsb'])
    if M3_STEPS < 2:
        return
    for t in range(NTILE):
        pb = t // 16
        fw.op('pe', lambda t=t, pb=pb: nc.tensor.matmul(kb.ps[pb][:, (t % 16) * 32:(t % 16 + 1) * 32], lhsT=gsb[0:32, t * 128:(t + 1) * 128], rhs=cst['ident'][0:32, 0:32], start=True, stop=True),
              reads=['gsb'], writes=[('ps', pb)])
    for pb, (t0, t1) in enumerate(((0, 16), (16, 32), (32, 34))):
        fw.op('dve', lambda pb=pb, t0=t0, t1=t1: nc.vector.tensor_copy(out=G[:, t0:t1, :], in_=kb.ps[pb][:, 0:(t1 - t0) * 32].rearrange("p (t c) -> p t c", c=32)),
              reads=[('ps', pb)], writes=[('G', pb)])
    gk = [('G', i) for i in range(3)]
    if M3_STEPS < 3:
        return
    fw.op('act', lambda: nc.scalar.activation(out=NLF[:], in_=G[:, :, 16:32], func=AF.Exp, scale=-1.0), reads=gk, writes=['NLF'])
    fw.op('act', lambda: nc.scalar.activation(out=NLF[:], in_=NLF[:], func=AF.Ln, bias=one[:], scale=1.0), reads=['NLF', 'one'], writes=['NLF'])
    if M3_STEPS < 4:
        return
    NLd = kb.sb('NLd', [128, 2, NTILE, 8], F32)
    NBs = kb.sb('NBs', [128, 2, NTILE, 8], F32)
    NEs = kb.sb('NEs', [128, 2, NTILE, 8], F32)
    for d in range(2):
        fw.op('dve', lambda d=d: nc.vector.tensor_copy(out=NLd[:, d, :, :], in_=NLF[:, :, d * 8:(d + 1) * 8]), reads=['NLF'], writes=[('NLd', d)])
        fw.op('pe', lambda d=d: nc.tensor.matmul(kb.ps[4 + d][:, 0:NTILE * 8], lhsT=tri[:, d, :], rhs=NLd[:, d, :, :].rearrange("p t c -> p (t c)"), start=True, stop=True),
              reads=[('NLd', d), 'tri'], writes=[('ps', 4 + d)])
        fw.op('pe', lambda d=d: nc.tensor.matmul(kb.ps[6 + d][:, 0:NTILE * 8], lhsT=cst['ones'][:], rhs=NLd[:, d, :, :].rearrange("p t c -> p (t c)"), start=True, stop=True),
              reads=[('NLd', d)], writes=[('ps', 6 + d)])
        fw.op('dve', lambda d=d: nc.vector.tensor_copy(out=NBs[:, d, :, :].rearrange("p t c -> p (t c)"), in_=kb.ps[4 + d][:, 0:NTILE * 8]), reads=[('ps', 4 + d)], writes=[('NBs', d)])
        fw.op('dve', lambda d=d: nc.vector.tensor_copy(out=NEs[:, d, :, :].rearrange("p t c -> p (t c)"), in_=kb.ps[6 + d][:, 0:NTILE * 8]), reads=[('ps', 6 + d)], writes=[('NEs', d)])
    if M3_STEPS < 5:
        return
    for d in range(2):
        nb = NBs[:, d, :, :]
        ne = NEs[:, d, :, :]
        sl = slice(d * 8, (d + 1) * 8)
        fw.op('act', lambda nb=nb, sl=sl: nc.scalar.activation(out=T['EQ'][:, :, sl], in_=nb, func=AF.Exp, scale=-1.0), reads=[('NBs', d)], writes=[('EQ', d)])
        fw.op('dve', lambda nb=nb, sl=sl: nc.vector.tensor_tensor(out=A[:, :, sl], in0=nb, in1=G[:, :, sl], op=ALU.add), reads=[('NBs', d)] + gk, writes=[('A', d)])
        fw.op('act', lambda sl=sl: nc.scalar.activation(out=T['EKS'][:, :, sl], in_=A[:, :, sl], func=AF.Exp), reads=[('A', d)], writes=[('EKS', d)])
        fw.op('dve', lambda sl=sl: nc.vector.tensor_scalar_mul(out=T['EKS'][:, :, sl], in0=T['EKS'][:, :, sl], scalar1=1.0 / 16), reads=[('EKS', d)], writes=[('EKS', d)])
        fw.op('act', lambda ne=ne, sl=sl: nc.scalar.activation(out=T['E'][:, :, sl], in_=ne, func=AF.Exp, scale=-1.0), reads=[('NEs', d)], writes=[('E', d)])
        fw.op('dve', lambda sl=sl: nc.vector.tensor_tensor(out=T['EKW'][:, :, sl], in0=T['EKS'][:, :, sl], in1=T['E'][:, :, sl], op=ALU.mult), reads=[('EKS', d), ('E', d)], writes=[('EKW', d)])


def phase_M4(kb, io, cst, S, T):
    nc, fw = kb.nc, kb.fw
    tri = kb.sb('tri', [128, 2, 128], F32)
    skp = kb.sb('skp', [128, 16], F32)
    mhg = kb.sb('mhg', [128, 16], F32)
    fw.dma('sp', tri[:], io['tri'].rearrange("d p n -> p d n"), writes=['tri'])
    fw.dma('sp', skp[:], io['skipT'], writes=['skp'])
    fw.dma('sp', mhg[:], io['mhgT'], writes=['mhg'])
    qTh = kb.sb('qTh', [128, 2, NT], BF16)
    kTh = kb.sb('kTh', [128, 2, NT], BF16)
    ktk = kb.sb('ktk', [128, NTILE, 256], BF16)
    vext = kb.sb('vext', [128, NTILE, 257], BF16)
    xcTh = kb.sb('xcTh', [128, 2, NL], BF16)
    szTh = kb.sb('szTh', [128, 2, NL], BF16)
    hbuf = kb.sb('hbuf', [128, 32, 256], F32)
    yTh = kb.sb('yTh', [128, 2, NL], BF16)
    Cf = [kb.sb(f'Cf{d}', [128, 2, 257], F32) for d in range(2)]
    Cb = [kb.sb(f'Cb{d}', [128, 2, 257], BF16) for d in range(2)]
    sm = [kb.sb(f'sm{d}', [128, 128], BF16) for d in range(2)]
    vw2 = [[kb.sb(f'vw{d}{i}', [128, 257], BF16) for i in range(2)] for d in range(2)]
    nd = [kb.sb(f'nd{d}', [128, 257], F32) for d in range(2)]
    dn = [kb.sb(f'dn{d}', [128, 2], F32) for d in range(2)]
    hs = kb.sb('hs', [128, 256], F32)
    st = kb.sb('st', [128, 6], F32)
    mv = kb.sb('mv', [128, 2], F32)
    rs = kb.sb('rs', [128, 1], F32)
    tmp = [kb.sb(f'tmp{j}', [128, 128], F32) for j in range(4)]
    st4 = kb.sb('st4', [128, 4, 6], F32)
    mv4 = [kb.sb(f'mv4{i}', [128, 4, 2], F32) for i in range(2)]
    rs4 = [kb.sb(f'rs4{i}', [128, 4], F32) for i in range(2)]
    fw.op('pool', lambda: nc.gpsimd.memset(vext[:], 1.0), writes=['vext'])
    order = [[32, 33] + list(range(32)), [33, 32] + list(range(31, -1, -1))]
    def load_scan(h):
        fw.dma('sp', qTh[:], S['qT1'][h * 256:(h + 1) * 256, :].rearrange("(j p) n -> p j n", p=128), writes=['qTh'])
        fw.dma('act', kTh[:], S['kT1'][h * 256:(h + 1) * 256, :].rearrange("(j p) n -> p j n", p=128), writes=['kTh'])
        fw.dma('sp', ktk[:], S['ktok'][:, h * 256:(h + 1) * 256].rearrange("(t p) c -> p t c", p=128), writes=['ktk'])
        fw.dma('pool', vext[:, :, 0:256], S['vtok'][:, h * 256:(h + 1) * 256].rearrange("(t p) c -> p t c", p=128), writes=['vext'])

    load_scan(0)
    for h in range(8):
        fw.dma('sp', xcTh[:], S['xcT'][h * 256:(h + 1) * 256, :].rearrange("(j p) n -> p j n", p=128), writes=['xcTh'])
        fw.dma('sp', szTh[:], S['szT'][h * 256:(h + 1) * 256, :].rearrange("(j p) n -> p j n", p=128), writes=['szTh'])
        for d in range(2):
            for j in range(2):
                fw.op('pool', lambda d=d, j=j: nc.gpsimd.memset(Cf[d][:, j, :], 0.0), writes=[('Cf', d, j)])
            fw.op('pool', lambda d=d: nc.gpsimd.memset(Cb[d][:], 0.0), writes=[('Cb', d)])
        def stV(step, h=h):
            if step >= NTILE - 1:
                return
            for d in range(2):
                c = order[d][step]
                col = d * 8 + h
                vb = vw2[d][step % 2]
                fw.op('act', lambda d=d, c=c, col=col, vb=vb: nc.scalar.activation(out=vb[:], in_=vext[:, c, :], func=AF.Identity, scale=T['EKW'][:, c, col:col + 1]), reads=['vext'], writes=[('vw', d, step % 2)])

        def stS(step, h=h):
            for d in range(2):
                c = order[d][step]
                col = d * 8 + h
                cs = slice(c * 128, (c + 1) * 128)
                if c < 32:
                    for j in range(2):
                        fw.op('pe', lambda j=j, d=d, cs=cs: nc.tensor.matmul(kb.ps[d][:, 0:128], lhsT=kTh[:, j, cs], rhs=qTh[:, j, cs], start=(j == 0), stop=(j == 1)),
                              reads=['qTh', 'kTh'], writes=[('ps', d)], signal=(j == 1))
                    fw.op('dve', lambda d=d, c=c, col=col: nc.vector.scalar_tensor_tensor(out=sm[d][:], in0=kb.ps[d][:, 0:128], scalar=T['EKS'][:, c, col:col + 1], in1=tri[:, d, :],
                                                                                          op0=ALU.mult, op1=ALU.mult), reads=[('ps', d), 'tri'], writes=[('sm', d)])

        def stD(step, h=h):
            if step >= NTILE - 1:
                return
            for d in range(2):
                c = order[d][step]
                vb = vw2[d][step % 2]
                for j in range(2):
                    bk = 4 + 2 * j + d
                    fw.op('pe', lambda j=j, bk=bk, c=c, vb=vb: nc.tensor.matmul(kb.ps[bk][:, 0:257], lhsT=ktk[:, c, j * 128:(j + 1) * 128], rhs=vb[:], start=True, stop=True),
                          reads=[('vw', d, step % 2), 'ktk'], writes=[('ps', bk)])

        def stB(step, h=h):
            for d in range(2):
                c = order[d][step]
                col = d * 8 + h
                cs = slice(c * 128, (c + 1) * 128)
                if c < 32:
                    nbk = 2 + d
                    for j in range(2):
                        fw.op('pe', lambda j=j, nbk=nbk, cs=cs, d=d: nc.tensor.matmul(kb.ps[nbk][:, 0:257], lhsT=qTh[:, j, cs], rhs=Cb[d][:, j, :], start=(j == 0), stop=False),
                              reads=[('Cb', d), 'qTh'], writes=[('ps', nbk)], signal=False)
                    fw.op('pe', lambda nbk=nbk, c=c, d=d: nc.tensor.matmul(kb.ps[nbk][:, 0:257], lhsT=sm[d][:], rhs=vext[:, c, :], start=False, stop=True),
                          reads=[('sm', d), 'vext'], writes=[('ps', nbk)])
                    fw.op('act', lambda nbk=nbk, d=d, c=c, col=col: nc.scalar.activation(out=nd[d][:], in_=kb.ps[nbk][:, 0:257], func=AF.Identity, scale=T['EQ'][:, c, col:col + 1]),
                          reads=[('ps', nbk)], writes=[('nd', d)])

        def stU(step, h=h):
            if step >= NTILE - 1:
                return
            for d in range(2):
                c = order[d][step]
                col = d * 8 + h
                fw.op('dve', lambda d=d, c=c, col=col: nc.vector.scalar_tensor_tensor(out=Cf[d][:], in0=Cf[d][:], scalar=T['E'][:, c, col:col + 1], in1=kb.psall[:, 4 + d:8:2, 0:257],
                                                                                      op0=ALU.mult, op1=ALU.add), reads=[('ps', 4 + d), ('ps', 6 + d)], writes=[('Cf', d, 0), ('Cf', d, 1)])
                fw.op('act', lambda d=d: nc.scalar.copy(out=Cb[d][:], in_=Cf[d][:]), reads=[('Cf', d, 0), ('Cf', d, 1)], writes=[('Cb', d)])

        def stO(step, h=h):
            for d in range(2):
                c = order[d][step]
                if c >= 32:
                    continue
                cs = slice(c * 128, (c + 1) * 128)
                sbk = d
                fw.op('dve', lambda d=d: nc.vector.scalar_tensor_tensor(out=dn[d][:, 0:1], in0=nd[d][:, 256:257], scalar=-1.0, in1=nd[d][:, 256:257], op0=ALU.mult, op1=ALU.max), reads=[('nd', d)], writes=[('dn', d)])
                fw.op('dve', lambda d=d: nc.vector.tensor_scalar_max(out=dn[d][:, 0:1], in0=dn[d][:, 0:1], scalar1=1.0), reads=[('dn', d)], writes=[('dn', d)])
                fw.op('dve', lambda d=d: nc.vector.reciprocal(out=dn[d][:, 1:2], in_=dn[d][:, 0:1]), reads=[('dn', d)], writes=[('dn', d)])
                first = (c < 16) if d == 0 else (c >= 16)
                if first:
                    fw.op('dve', lambda d=d, c=c: nc.vector.tensor_scalar_mul(out=hbuf[:, c, :], in0=nd[d][:, 0:256], scalar1=dn[d][:, 1:2]), reads=[('nd', d), ('dn', d)], writes=[('hbuf', c)])
                    continue
                fw.op('dve', lambda d=d, c=c: nc.vector.scalar_tensor_tensor(out=hbuf[:, c, :], in0=nd[d][:, 0:256], scalar=dn[d][:, 1:2], in1=hbuf[:, c, :], op0=ALU.mult, op1=ALU.add),
                      reads=[('nd', d), ('dn', d)], writes=[('hbuf', c)])

        def fin_stats(g):
            for i in range(4):
                c = g * 4 + i
                fw.op('dve', lambda c=c, i=i: nc.vector.bn_stats(out=st4[:, i, :], in_=hbuf[:, c, :]), reads=[('hbuf', c)], writes=[('st4', i)])
                fw.op('dve', lambda i=i: nc.vector.bn_aggr(out=mv4[g % 2][:, i, :], in_=st4[:, i, :]), reads=[('st4', i)], writes=[('mv4', g % 2, i)])
            fw.op('act', lambda: nc.scalar.activation(out=rs4[g % 2][:], in_=mv4[g % 2][:, :, 1], func=AF.Ln, bias=cst['eps'][:], scale=1.0), reads=[('mv4', g % 2, i) for i in range(4)], writes=[('rs4', g % 2)])
            fw.op('act', lambda: nc.scalar.activation(out=rs4[g % 2][:], in_=rs4[g % 2][:], func=AF.Exp, scale=-0.5), reads=[('rs4', g % 2)], writes=[('rs4', g % 2)])

        def fin_rest(g, h=h):
            for i in range(4):
                c = g * 4 + i
                cs = slice(c * 128, (c + 1) * 128)
                fw.op('dve', lambda c=c, i=i: nc.vector.tensor_scalar(out=hbuf[:, c, :], in0=hbuf[:, c, :], scalar1=mv4[g % 2][:, i, 0:1], scalar2=rs4[g % 2][:, i:i + 1], op0=ALU.subtract, op1=ALU.mult),
                      reads=[('mv4', g % 2, i), ('rs4', g % 2)], writes=[('hbuf', c)])
                pbk = (g * 4 + i) % 4
                for j in range(2):
                    fw.op('pe', lambda j=j, c=c, pbk=pbk: nc.tensor.transpose(out=kb.ps[pbk][:, j * 128:(j + 1) * 128], in_=hbuf[:, c, j * 128:(j + 1) * 128], identity=cst['ident'][:]),
                          reads=[('hbuf', c)], writes=[('ps', pbk)])
                for j in range(2):
                    tb = (i * 2 + j) % 4
                    fw.op('dve', lambda j=j, pbk=pbk, cs=cs, tb=tb: nc.vector.scalar_tensor_tensor(out=tmp[tb][:], in0=kb.ps[pbk][:, j * 128:(j + 1) * 128], scalar=mhg[:, 2 * h + j:2 * h + j + 1],
                                                                                                   in1=xcTh[:, j, cs], op0=ALU.mult, op1=ALU.add),
                          reads=[('ps', pbk), 'xcTh', 'mhg'], writes=[('tmp', tb)])
                    fw.op('pool', lambda j=j, cs=cs, tb=tb: nc.gpsimd.tensor_tensor(out=yTh[:, j, cs], in0=tmp[tb][:], in1=szTh[:, j, cs], op=ALU.mult), reads=[('tmp', tb), 'szTh'], writes=['yTh'])

        stV(0)
        stS(0)
        stD(0)
        for step in range(NTILE):
            if step + 1 < NTILE:
                stV(step + 1)
            stB(step)
            stU(step)
            if step + 1 < NTILE:
                stS(step + 1)
                stD(step + 1)
            stO(step)
        if h + 1 < 8:
            load_scan(h + 1)
        for j in range(2):
            fw.op('dve', lambda j=j, h=h: nc.vector.tensor_scalar_mul(out=xcTh[:, j, :], in0=xcTh[:, j, :], scalar1=skp[:, 2 * h + j:2 * h + j + 1]), reads=['xcTh', 'skp'], writes=['xcTh'])
        fin_stats(0)
        for g in range(8):
            if g + 1 < 8:
                fin_stats(g + 1)
            fin_rest(g)
        fw.dma('sp', S['yT1'][h * 256:(h + 1) * 256, :].rearrange("(j p) n -> p j n", p=128), yTh[:], reads=['yTh'])


_NC_CACHE = {}


def kernel(**inputs):
    inp = {k: np.asarray(v) for k, v in inputs.items()}
    sh = prep_shared(inp)
    if 'nc' not in _NC_CACHE:
        _NC_CACHE['nc'] = build()
    nc = _NC_CACHE['nc']
    in_maps = []
    for b in range(8):
        m = dict(sh)
        m.update(prep_core(inp, b))
        in_maps.append(m)
    res = run_bass_kernel_spmd(nc, in_maps, core_ids=list(range(8)))
    out = np.stack([np.asarray(r['out'], dtype=np.float32) for r in res.results], 0)
    return out
```
